# Optimizing a Trainium2 kernel written in Bass

```python
import jax, jax.numpy as jnp
from jax import lax
import numpy as np

D_MODEL = 1024
BATCH = 4
SEQ = 8192
DEPTH = 1

CHUNK = 64
Q_BLOCK = 128
EPS = 1e-6
MLA_HEADS = 8
MLA_NOPE = 64
MLA_ROPE = 32
MLA_V = 64
MLA_Q_RANK = 384
MLA_KV_RANK = 256
ROPE_BASE = 10000.0
MLA_WIDTH = MLA_HEADS * MLA_V
DSA_HEADS = 8
DSA_HEAD_DIM = 64
DSA_WIDTH = DSA_HEADS * DSA_HEAD_DIM
IDX_HEADS = 8
IDX_DIM = 32
TOPK_MAX = 256
REL_BUCKETS = 32
REL_MAX_DIST = 128

IN_SPLITS = (MLA_Q_RANK, MLA_KV_RANK, MLA_ROPE, MLA_WIDTH,
             DSA_WIDTH, DSA_WIDTH, DSA_WIDTH, DSA_WIDTH,
             IDX_HEADS * IDX_DIM, IDX_DIM, IDX_HEADS,
             D_MODEL, D_MODEL)
IN_TOTAL = sum(IN_SPLITS)

kernel_name = 'hybrid_mla_dsa_gated_parallel'


def rmsnorm(x, g):
    xf = x.astype(jnp.float32)
    y = xf * lax.rsqrt(jnp.mean(xf * xf, axis=-1, keepdims=True) + EPS)
    return (y * g.astype(jnp.float32)).astype(x.dtype)


def rope(x, pos):
    half = x.shape[-1] // 2
    freqs = ROPE_BASE ** (-jnp.arange(half, dtype=jnp.float32) / half)
    ang = pos.astype(jnp.float32)[:, None] * freqs[None, :]
    cos = jnp.cos(ang)[None, :, None, :].astype(x.dtype)
    sin = jnp.sin(ang)[None, :, None, :].astype(x.dtype)
    x1, x2 = x[..., :half], x[..., half:]
    return jnp.concatenate([x1 * cos - x2 * sin, x1 * sin + x2 * cos], axis=-1)


def t5_bucket(rel):
    nb = REL_BUCKETS // 2
    max_exact = nb // 2
    ret = (rel > 0).astype(jnp.int32) * nb
    n = jnp.abs(rel)
    nf = jnp.maximum(n, 1).astype(jnp.float32)
    large = max_exact + (jnp.log(nf / max_exact) / np.log(REL_MAX_DIST / max_exact)
                         * (nb - max_exact)).astype(jnp.int32)
    large = jnp.minimum(large, nb - 1)
    return ret + jnp.where(n < max_exact, n, large)


def to_blocks(a):
    return jnp.moveaxis(a.reshape((a.shape[0], -1, Q_BLOCK) + a.shape[2:]), 1, 0)


def from_blocks(a):
    a = jnp.moveaxis(a, 0, 1)
    return a.reshape((a.shape[0], -1) + a.shape[3:])


def mla_attention(q_nope, q_pe, k_nope, k_pe, v, pos):
    scale = (MLA_NOPE + MLA_ROPE) ** -0.5
    key_chunk = pos // CHUNK

    def block(args):
        qn, qp, qpos = args
        logits = (jnp.einsum('bqhd,bkhd->bhqk', qn, k_nope)
                  + jnp.einsum('bqhr,bkr->bhqk', qp, k_pe))
        logits = logits.astype(jnp.float32) * scale
        mask = key_chunk[None, :] <= (qpos // CHUNK)[:, None]
        logits = jnp.where(mask[None, None], logits, -jnp.inf)
        p = jax.nn.softmax(logits, axis=-1).astype(v.dtype)
        return jnp.einsum('bhqk,bkhd->bqhd', p, v)

    out = lax.map(block, (to_blocks(q_nope), to_blocks(q_pe), pos.reshape(-1, Q_BLOCK)))
    return from_blocks(out)


def dsa_attention(q, k, v, q_idx, k_idx, w_idx, rel_bias, pos, topk):
    scale = DSA_HEAD_DIM ** -0.5
    key_chunk = pos // CHUNK
    gather = jax.vmap(lambda src, ids: src[ids])

    def block(args):
        qb, qi, wi, qpos = args
        qchunk = qpos // CHUNK
        s = jnp.einsum('bqhd,bkd->bqhk', qi, k_idx).astype(jnp.float32) * (IDX_DIM ** -0.5)
        score = jnp.einsum('bqhk,bqh->bqk', jax.nn.relu(s),
                           wi.astype(jnp.float32) * (IDX_HEADS ** -0.5))
        admissible = key_chunk[None, :] <= qchunk[:, None]
        score = jnp.where(admissible[None], score, -jnp.inf)
        _, idx = lax.top_k(score, topk)
        valid = (idx // CHUNK) <= qchunk[None, :, None]
        k_sel = gather(k, idx)
        v_sel = gather(v, idx)
        logits = jnp.einsum('bqhd,bqkhd->bhqk', qb, k_sel).astype(jnp.float32) * scale
        bias = rel_bias[t5_bucket(idx - qpos[None, :, None])]
        logits = logits + jnp.transpose(bias, (0, 3, 1, 2)).astype(jnp.float32)
        logits = jnp.where(valid[:, None], logits, -jnp.inf)
        p = jax.nn.softmax(logits, axis=-1).astype(v.dtype)
        return jnp.einsum('bhqk,bqkhd->bqhd', p, v_sel)

    out = lax.map(block, (to_blocks(q), to_blocks(q_idx), to_blocks(w_idx),
                          pos.reshape(-1, Q_BLOCK)))
    return from_blocks(out)


def setup_inputs(seed: int = 0) -> dict:
    key = jax.random.key(seed)
    ks = jax.random.split(key, 13)

    def dense(k, shape):
        return jax.random.normal(k, shape, jnp.float32) * shape[-2] ** -0.5

    def gain(k, shape):
        return 1.0 + 0.05 * jax.random.normal(k, shape, jnp.float32)

    return {
        'x': jax.random.normal(ks[0], (BATCH, SEQ, D_MODEL), jnp.float32),
        'norm_g': gain(ks[1], (DEPTH, D_MODEL)),
        'w_in': dense(ks[2], (DEPTH, D_MODEL, IN_TOTAL)),
        'g_q_lat': gain(ks[3], (DEPTH, MLA_Q_RANK)),
        'w_uq': dense(ks[4], (DEPTH, MLA_Q_RANK, MLA_HEADS * (MLA_NOPE + MLA_ROPE))),
        'g_kv_lat': gain(ks[5], (DEPTH, MLA_KV_RANK)),
        'w_ukv': dense(ks[6], (DEPTH, MLA_KV_RANK, MLA_HEADS * (MLA_NOPE + MLA_V))),
        'w_o_a': dense(ks[7], (DEPTH, MLA_WIDTH, D_MODEL)),
        'w_o_b': dense(ks[8], (DEPTH, DSA_WIDTH, D_MODEL)),
        'w_out': dense(ks[9], (DEPTH, D_MODEL, D_MODEL)),
        'rel_bias': 0.5 * jax.random.normal(ks[10], (REL_BUCKETS, DSA_HEADS), jnp.float32),
        'final_g': gain(ks[11], (D_MODEL,)),
    }


def reference(x, norm_g, w_in, g_q_lat, w_uq, g_kv_lat, w_ukv, w_o_a, w_o_b, w_out,
              rel_bias, final_g):
    B, S, _ = x.shape
    pos = jnp.arange(S, dtype=jnp.int32)
    topk = min(TOPK_MAX, S // 4)
    cuts = np.cumsum(IN_SPLITS)[:-1].tolist()
    for l in range(DEPTH):
        h = rmsnorm(x, norm_g[l])
        (q_lat, c_kv, k_rope, z_a, q_b, k_b, v_b, z_b,
         q_idx, k_idx, w_idx, gate_a, gate_b) = jnp.split(h @ w_in[l], cuts, axis=-1)

        q = (rmsnorm(q_lat, g_q_lat[l]) @ w_uq[l]).reshape(B, S, MLA_HEADS, MLA_NOPE + MLA_ROPE)
        q_nope, q_pe = q[..., :MLA_NOPE], rope(q[..., MLA_NOPE:], pos)
        kv = (rmsnorm(c_kv, g_kv_lat[l]) @ w_ukv[l]).reshape(B, S, MLA_HEADS, MLA_NOPE + MLA_V)
        k_nope, v_a = kv[..., :MLA_NOPE], kv[..., MLA_NOPE:]
        k_pe = rope(k_rope[:, :, None, :], pos)[:, :, 0, :]
        y_a = mla_attention(q_nope, q_pe, k_nope, k_pe, v_a, pos).reshape(B, S, MLA_WIDTH)
        y_a = y_a * jax.nn.silu(z_a)

        y_b = dsa_attention(q_b.reshape(B, S, DSA_HEADS, DSA_HEAD_DIM),
                            k_b.reshape(B, S, DSA_HEADS, DSA_HEAD_DIM),
                            v_b.reshape(B, S, DSA_HEADS, DSA_HEAD_DIM),
                            q_idx.reshape(B, S, IDX_HEADS, IDX_DIM), k_idx, w_idx,
                            rel_bias, pos, topk).reshape(B, S, DSA_WIDTH)
        y_b = y_b * jax.nn.silu(z_b)

        merged = (jax.nn.sigmoid(gate_a) * (y_a @ w_o_a[l])
                  + jax.nn.sigmoid(gate_b) * (y_b @ w_o_b[l]))
        x = x + merged @ w_out[l]
    return rmsnorm(x, final_g)
```

```python
import numpy as np
from contextlib import ExitStack
import concourse.bass as bass
import concourse.mybir as mybir
from concourse.bass_utils import run_bass_kernel_spmd

F32 = mybir.dt.float32
BF16 = mybir.dt.bfloat16
AF = mybir.ActivationFunctionType
ALU = mybir.AluOpType
EPOCH = 20000
D = 1024
EPS = 1e-6
NB = 14
TOPK = 256.0
NEG = -30000.0
C_QLAT, C_CKV, C_KROPE, C_ZA, C_QB, C_KB, C_VB, C_ZB, C_QI, C_KI, C_WI, C_GA, C_GB = (
    0, 384, 640, 672, 1184, 1696, 2208, 2720, 3232, 3488, 3520, 3528, 4552)
C_KROT = 5576
NCOL = 5608


class Res:
    __slots__ = ("name", "lw", "rd")

    def __init__(self, name=""):
        self.name = name
        self.lw = None
        self.rd = []


class Op:
    __slots__ = ("eng", "fn", "deps", "inc", "tok", "dma", "tag")


class Ctx:
    ENG = ("pe", "act", "dve", "pool", "sp")
    NEP = {"pe": 6, "act": 3, "dve": 4, "pool": 2, "sp": 1}

    def __init__(self, nc, st):
        self.nc = nc
        self.esem = {e: [st.enter_context(nc.semaphore("s_%s%d" % (e, i))) for i in range(self.NEP[e])] for e in self.ENG}
        self.dsem_pool = [st.enter_context(nc.semaphore("s_d%d" % i)) for i in range(84)]
        self.dsem = {}
        self.dcnt = {}
        self.seq = {e: 0 for e in self.ENG}
        self.dma_hist = []

    def dma_sem(self, tag):
        if tag not in self.dsem:
            self.dsem[tag] = self.dsem_pool[len(self.dsem)]
            self.dcnt[tag] = 0
        return self.dsem[tag]


class Sched:
    def __init__(self, ctx):
        self.ctx = ctx
        self.ops = []

    def op(self, eng, fn, reads=(), writes=(), dma_tag=None):
        o = Op()
        o.eng, o.fn, o.inc, o.tok = eng, fn, False, None
        o.dma = dma_tag is not None
        o.tag = dma_tag
        idx = len(self.ops)
        deps = set()
        for r in reads:
            if r.lw is not None:
                deps.add(r.lw)
        for w in writes:
            if w.lw is not None:
                deps.add(w.lw)
            deps.update(w.rd)
        for r in reads:
            r.rd.append(idx)
        for w in writes:
            w.lw = idx
            w.rd = []
        if o.dma:
            hist = self.ctx.dma_hist
            if len(hist) >= 4 and hist[-4][0] is self:
                deps.add(hist[-4][1])
            hist.append((self, idx))
        deps.discard(idx)
        o.deps = deps
        self.ops.append(o)
        return o

    def emit(self, resources):
        import os
        ctx = self.ctx
        ctx.phase_no = getattr(ctx, "phase_no", 0) + 1
        if str(ctx.phase_no) not in os.environ.get("KPHASES", "1234"):
            for r in resources:
                r.lw = None
                r.rd = []
            return
        nc = ctx.nc
        ops = self.ops
        for o in ops:
            best = {}
            keep = set()
            for d in o.deps:
                p = ops[d]
                if p.dma:
                    keep.add(d)
                    continue
                if p.eng == "pe" and o.eng == "pe" and not o.dma:
                    continue
                if best.get(p.eng, -1) < d:
                    best[p.eng] = d
            keep.update(best.values())
            o.deps = keep
            for d in keep:
                ops[d].inc = True
            if o.dma:
                o.inc = True
        for o in ops:
            if not o.inc:
                continue
            if o.dma:
                sem = ctx.dma_sem(o.tag)
                ctx.dcnt[o.tag] += 1
                o.tok = (sem, 16 * ctx.dcnt[o.tag])
            else:
                s = ctx.seq[o.eng]
                ctx.seq[o.eng] += 1
                o.tok = (ctx.esem[o.eng][s // EPOCH], (s % EPOCH) + 1)
        per_eng = {e: [o for o in ops if o.eng == e] for e in Ctx.ENG}
        import os
        if os.environ.get("KDEBUG"):
            print("phase ops", {e: (len(per_eng[e]), sum(1 for o in per_eng[e] if o.inc)) for e in Ctx.ENG}, "dma tags", len(ctx.dsem), flush=True)
        with nc.Block() as block:
            handles = {"pe": block.tensor, "act": block.scalar, "dve": block.vector, "pool": block.gpsimd, "sp": block.sync}

            def make(e):
                def body(eng):
                    waited = {}
                    for o in per_eng[e]:
                        need = {}
                        for d in o.deps:
                            p = ops[d]
                            if p.tok is None:
                                continue
                            if p.eng == "pe" and e == "pe" and not p.dma and not o.dma:
                                continue
                            k, v = p.tok
                            if need.get(id(k), (None, 0))[1] < v:
                                need[id(k)] = (k, v)
                        for kid, (k, v) in need.items():
                            if waited.get(kid, 0) >= v:
                                continue
                            eng.wait_ge(k, v)
                            waited[kid] = v
                        ins = o.fn(eng)
                        if o.inc:
                            ins.then_inc(o.tok[0], 16 if o.dma else 1)
                    last = {}
                    for o in per_eng[e]:
                        if o.dma:
                            last[id(o.tok[0])] = o.tok
                    for kid, (k, v) in last.items():
                        if waited.get(kid, 0) < v:
                            eng.wait_ge(k, v)

                return body

            for e in Ctx.ENG:
                if per_eng[e]:
                    handles[e](make(e))
        for r in resources:
            r.lw = None
            r.rd = []


def build_nc(S):
    NG = S // 512
    NT = S // 128
    NOG = NG // 2
    NO = NOG * 512
    NOT_ = NOG * 4
    HS = min(2048, S)

    nc = bass.Bass("TRN2", target_bir_lowering=False)
    din = lambda n, s, d=F32: nc.dram_tensor(n, s, d, kind="ExternalInput").ap()
    dscr = lambda n, s, d=BF16: nc.dram_tensor(n, s, d, kind="Internal").ap()
    xT = din("xT", [D, S])
    xTo = din("xTo", [D, NO])
    maskA_d = din("maskA", [128, 256])
    adm_d = din("adm", [128, 256])
    xo = din("xo", [NO, D])
    w_in = din("w_in", [D, NCOL])
    w_uqa = din("w_uqa", [384, 768])
    w_uqb = din("w_uqb", [384, 768])
    w_uk = din("w_uk", [256, 512])
    w_uv = din("w_uv", [256, 512])
    w_oa = din("w_oa", [512, D])
    w_ob = din("w_ob", [512, D])
    w_out = din("w_out", [D, D])
    g_in = din("g_in", [128, 8])
    g_q = din("g_q", [128, 3])
    g_kv = din("g_kv", [128, 2])
    fg_b = din("fg_b", [128, D])
    ident_d = din("ident", [128, 128])
    pw_d = din("pw", [128, 2 * (NB + 1)])
    cq_d = din("cq", [96, NO])
    sq_d = din("sq", [96, NO])
    ck_d = din("ck", [32, S])
    sk_d = din("sk", [32, S])
    biasT_d = din("biasT", [128, 8 * 3 * 128])
    rb15_d = din("rb15", [128, 8])
    out_d = nc.dram_tensor("out", [NO, D], F32, kind="ExternalOutput").ap()
    knT = dscr("knT", [8, 64, S])
    kpeT = dscr("kpeT", [32, S])
    vA = dscr("vA", [S, 520])
    vB = dscr("vB", [S, 520])
    kbT = dscr("kbT", [4, 128, S])
    kiT = dscr("kiT", [32, S])
    qaT = dscr("qaT", [8, 96, NO])
    qbT = dscr("qbT", [4, 128, NO])
    qiT = dscr("qiT", [8, 32, NO])
    wiD = dscr("wiD", [NO, 8], F32)
    zg = dscr("zg", [NO, 3072])
    yaD = dscr("yaD", [NO, 512])
    ybD = dscr("ybD", [NO, 512])

    with ExitStack() as top:
        ctx = Ctx(nc, top)
        allres = []

        def R(name=""):
            x = Res(name)
            allres.append(x)
            return x

        dram_res = {k: R(k) for k in ["knT", "kpeT", "vA", "vB", "kbT", "kiT", "qaT", "qbT", "qiT", "wiD", "zg", "yaD", "ybD", "out"]}

        with ExitStack() as st:
            sb = lambda n, s, d: st.enter_context(nc.sbuf_tensor("sb_" + n, s, d))
            Sx = Sched(ctx)
            Wb = sb("Wb", [128, 8, NCOL], BF16)
            wst = [sb("wst%d" % i, [128, 768], F32) for i in range(1)] * 2
            wst_r = [R()] * 2
            Wb_r = R()
            WqA = sb("WqA", [128, 3, 768], BF16)
            WqB = sb("WqB", [128, 3, 768], BF16)
            Wk = sb("Wk", [128, 2, 512], BF16)
            Wv = sb("Wv", [128, 2, 512], BF16)
            W2_r = R()
            gin = sb("gin", [128, 8], F32)
            gq = sb("gq", [128, 3], F32)
            gkv = sb("gkv", [128, 2], F32)
            g_r = R()
            ones = sb("ones", [128, 128], BF16)
            ones_r = R()
            xs = [sb("xs0", [128, 8, 512], F32)] * 2
            xs_r = [R()] * 2
            sq = sb("sqx", [128, 8, 512], BF16)
            sq_r = R()
            rt = sb("rt", [128, 512], F32)
            rt_r = R()
            R1 = sb("R1", [128, 512], F32)
            R1_r = R()
            hT = [sb("hT0", [128, 8, 512], BF16)] * 2
            hT_r = [R()] * 2
            raw = sb("raw", [128, 3, 512], F32)
            raw_r = R()
            sq2 = sb("sqx2", [128, 3, 512], BF16)
            sq2_r = R()
            rt2 = sb("rt2", [128, 512], F32)
            rt2_r = R()
            R2 = sb("R2", [128, 512], F32)
            R2_r = R()
            ckvn = sb("ckvn", [128, 2, 512], BF16)
            ckvn_r = R()
            qln = sb("qln", [128, 3, 512], BF16)
            qln_r = R()
            knst = sb("knst", [64, 4, 512], BF16)
            knst_r = R()
            vst = [sb("vst%d" % i, [128, 4, 8, 65], BF16) for i in range(2)]
            vst_r = [R() for _ in range(2)]
            kbst = sb("kbst", [128, 4, 512], BF16)
            kbst_r = R()
            kist = sb("kist", [32, 512], BF16)
            kist_r = R()
            kpst = sb("kpst", [32, 512], BF16)
            kpst_r = R()
            tb = [sb("tb%d" % i, [96, 512], F32) for i in range(4)]
            tb_r = [R() for _ in range(4)]
            t1 = sb("t1", [96, 512], F32)
            t1_r = R()
            t2 = sb("t2", [96, 512], F32)
            t2_r = R()
            qast = sb("qast", [96, 4, 512], BF16)
            qast_r = R()
            qbst = sb("qbst", [128, 4, 512], BF16)
            qbst_r = R()
            qist = sb("qist", [128, 2, 512], BF16)
            qist_r = R()
            zgst = sb("zgst", [128, 3072], BF16)
            zgst_r = R()
            wist = sb("wist", [128, 4, 8], F32)
            wist_r = R()
            ps = st.enter_context(nc.psum_tensor("ps1", [128, 8, 512], F32))
            ps_r = [R() for _ in range(8)]
            B_SS, B_SS2 = 0, 1
            fm_banks = [2, 3, 4]
            tm_banks = [5, 6, 7]
            cnt = {"fm": 0, "tm": 0, "ev": 0}

            def nfm():
                b = fm_banks[cnt["fm"] % 3]
                cnt["fm"] += 1
                return b

            def ntm():
                b = tm_banks[cnt["tm"] % 3]
                cnt["tm"] += 1
                return b

            Sx.op("pool", lambda e: e.memset(ones[:], 1.0), writes=[ones_r])
            Sx.op("sp", lambda e: e.dma_start(out=gin[:], in_=g_in), writes=[g_r], dma_tag="g0")
            Sx.op("sp", lambda e: e.dma_start(out=gq[:], in_=g_q), writes=[g_r], dma_tag="g1")
            Sx.op("sp", lambda e: e.dma_start(out=gkv[:], in_=g_kv), writes=[g_r], dma_tag="g2")
            for i in range(2):
                for k in range(4):
                    Sx.op("pool", lambda e, i=i, k=k: e.memset(vst[i][:, k, :, :], 1.0), writes=[vst_r[i]])
            w_in_v = w_in.rearrange("(c p) n -> p c n", p=128)
            k = 0
            for c in range(8):
                for n0 in range(0, NCOL, 768):
                    n1 = min(NCOL, n0 + 768)
                    sl = k % 2
                    Sx.op("sp", lambda e, c=c, n0=n0, n1=n1, sl=sl: e.dma_start(out=wst[sl][:, 0:n1 - n0], in_=w_in_v[:, c, n0:n1]),
                          writes=[wst_r[sl]], dma_tag="wst0")
                    eng = "dve" if k % 2 == 0 else "pool"
                    Sx.op(eng, lambda e, c=c, n0=n0, n1=n1, sl=sl: e.tensor_scalar(out=Wb[:, c, n0:n1], in0=wst[sl][:, 0:n1 - n0], scalar1=gin[:, c:c + 1], scalar2=None, op0=ALU.mult),
                          reads=[wst_r[sl], g_r], writes=[Wb_r])
                    k += 1

            def small_w(dst, src, nch, ncols, gain):
                nonlocal k
                v = src.rearrange("(c p) n -> p c n", p=128)
                for c in range(nch):
                    sl = k % 2
                    Sx.op("sp", lambda e, c=c, sl=sl: e.dma_start(out=wst[sl][:, 0:ncols], in_=v[:, c, :]), writes=[wst_r[sl]], dma_tag="wst0")
                    Sx.op("dve", lambda e, c=c, sl=sl: e.tensor_scalar(out=dst[:, c, :], in0=wst[sl][:, 0:ncols], scalar1=gain[:, c:c + 1], scalar2=None, op0=ALU.mult),
                          reads=[wst_r[sl], g_r], writes=[W2_r])
                    k += 1

            small_w(WqA, w_uqa, 3, 768, gq)
            small_w(WqB, w_uqb, 3, 768, gq)
            small_w(Wk, w_uk, 2, 512, gkv)
            small_w(Wv, w_uv, 2, 512, gkv)

            xT_v = xT.rearrange("(c p) t -> p c t", p=128)

            def norm_stats(src_raw_ap_list, nchunk, sq_t, sq_res, src_res, bank, dim, rt_t, rt_res, Rt, Rt_res):
                for c in range(nchunk):
                    Sx.op("act", lambda e, c=c: e.activation(out=sq_t[:, c, :], in_=src_raw_ap_list[c], func=AF.Square), reads=[src_res], writes=[sq_res])
                for c in range(nchunk):
                    Sx.op("pe", lambda e, c=c: e.matmul(ps[:, bank, :], lhsT=ones[:], rhs=sq_t[:, c, :], start=(c == 0), stop=(c == nchunk - 1)),
                          reads=[sq_res, ones_r], writes=[ps_r[bank]])
                Sx.op("act", lambda e: e.activation(out=rt_t[:], in_=ps[:, bank, :], func=AF.Sqrt, bias=EPS, scale=1.0 / dim), reads=[ps_r[bank]], writes=[rt_res])
                Sx.op("dve", lambda e: e.reciprocal(out=Rt[:], in_=rt_t[:]), reads=[rt_res], writes=[Rt_res])

            def fm_proj(h_t, h_res, col, M, nch, Wt, Wres):
                b = nfm()
                for c in range(nch):
                    Sx.op("pe", lambda e, c=c, b=b: e.matmul(ps[0:M, b, :], lhsT=Wt[:, c, col:col + M], rhs=h_t[:, c, :], start=(c == 0), stop=(c == nch - 1)),
                          reads=[h_res, Wres], writes=[ps_r[b]])
                return b

            def evac(b, M, out_ap, out_res, extra_reads=(), scale=None):
                eng = "act" if cnt["ev"] % 2 == 0 else "dve"
                cnt["ev"] += 1
                if scale is not None:
                    Sx.op("dve", lambda e: e.tensor_scalar(out=out_ap, in0=ps[0:M, b, :], scalar1=scale, scalar2=None, op0=ALU.mult), reads=[ps_r[b]] + list(extra_reads), writes=[out_res])
                elif eng == "act":
                    Sx.op("act", lambda e: e.activation(out=out_ap, in_=ps[0:M, b, :], func=AF.Copy), reads=[ps_r[b]] + list(extra_reads), writes=[out_res])
                else:
                    Sx.op("dve", lambda e: e.tensor_copy(out=out_ap, in_=ps[0:M, b, :]), reads=[ps_r[b]] + list(extra_reads), writes=[out_res])

            def rope_combine(bA, bB, M, ctab, stab, ctab_r, stab_r, out_ap, out_res):
                Sx.op("dve", lambda e: e.tensor_tensor(out=t1[0:M, :], in0=ps[0:M, bA, :], in1=ctab[0:M, :], op=ALU.mult), reads=[ps_r[bA], ctab_r], writes=[t1_r])
                Sx.op("dve", lambda e: e.tensor_tensor(out=t2[0:M, :], in0=ps[0:M, bB, :], in1=stab[0:M, :], op=ALU.mult), reads=[ps_r[bB], stab_r], writes=[t2_r])
                Sx.op("pool", lambda e: e.tensor_tensor(out=out_ap, in0=t1[0:M, :], in1=t2[0:M, :], op=ALU.add), reads=[t1_r, t2_r], writes=[out_res])

            def q_side(H, Hr, go):
                o0 = go * 512
                for c3 in range(3):
                    b = fm_proj(H, Hr, C_QLAT + c3 * 128, 128, 8, Wb, Wb_r)
                    evac(b, 128, raw[:, c3, :], raw_r)
                norm_stats([raw[:, c, :] for c in range(3)], 3, sq2, sq2_r, raw_r, B_SS2, 384.0, rt2, rt2_r, R2, R2_r)
                for c3 in range(3):
                    Sx.op("pool", lambda e, c3=c3: e.tensor_tensor(out=qln[:, c3, :], in0=raw[:, c3, :], in1=R2[:], op=ALU.mult), reads=[raw_r, R2_r], writes=[qln_r])
                for h in range(8):
                    bA = fm_proj(qln, qln_r, h * 96, 96, 3, WqA, W2_r)
                    bB = fm_proj(qln, qln_r, h * 96, 96, 3, WqB, W2_r)
                    rope_combine(bA, bB, 96, tb[2], tb[3], tb_r[2], tb_r[3], qast[:, h % 4, :], qast_r)
                    if h % 4 == 3:
                        Sx.op("sp", lambda e, o0=o0, h=h: e.dma_start(out=qaT.rearrange("h p t -> p h t")[:, h - 3:h + 1, o0:o0 + 512], in_=qast[:]), reads=[qast_r], writes=[dram_res["qaT"]], dma_tag="qast")
                for c4 in range(4):
                    b = fm_proj(H, Hr, C_QB + c4 * 128, 128, 8, Wb, Wb_r)
                    evac(b, 128, qbst[:, c4, :], qbst_r, scale=0.125)
                Sx.op("sp", lambda e, o0=o0: e.dma_start(out=qbT.rearrange("a p t -> p a t")[:, :, o0:o0 + 512], in_=qbst[:]), reads=[qbst_r], writes=[dram_res["qbT"]], dma_tag="qbst")
                for c2 in range(2):
                    b = fm_proj(H, Hr, C_QI + c2 * 128, 128, 8, Wb, Wb_r)
                    evac(b, 128, qist[:, c2, :], qist_r)
                Sx.op("sp", lambda e, o0=o0: e.dma_start(out=qiT.rearrange("(a hh) d t -> (hh d) a t", a=2)[:, :, o0:o0 + 512], in_=qist[:]), reads=[qist_r], writes=[dram_res["qiT"]], dma_tag="qist")
                for u in range(4):
                    for (cs, k6, fn) in [(C_ZA, 0, AF.Silu), (C_ZB, 1, AF.Silu), (C_GA, 2, AF.Sigmoid), (C_GA + 512, 3, AF.Sigmoid), (C_GB, 4, AF.Sigmoid), (C_GB + 512, 5, AF.Sigmoid)]:
                        b = ntm()
                        for c in range(8):
                            Sx.op("pe", lambda e, c=c, b=b, u=u, cs=cs: e.matmul(ps[:, b, :], lhsT=H[:, c, u * 128:(u + 1) * 128], rhs=Wb[:, c, cs:cs + 512], start=(c == 0), stop=(c == 7)),
                                  reads=[Hr, Wb_r], writes=[ps_r[b]])
                        Sx.op("act", lambda e, b=b, u=u, k6=k6, fn=fn: e.activation(out=zgst[:, k6 * 512:(k6 + 1) * 512], in_=ps[:, b, :], func=fn), reads=[ps_r[b]], writes=[zgst_r])
                    Sx.op("sp", lambda e, go=go, u=u: e.dma_start(out=zg[go * 512 + u * 128:go * 512 + (u + 1) * 128, :], in_=zgst[:]), reads=[zgst_r], writes=[dram_res["zg"]], dma_tag="zgst")
                    b = ntm()
                    for c in range(8):
                        Sx.op("pe", lambda e, c=c, b=b, u=u: e.matmul(ps[:, b, 0:8], lhsT=H[:, c, u * 128:(u + 1) * 128], rhs=Wb[:, c, C_WI:C_WI + 8], start=(c == 0), stop=(c == 7)),
                              reads=[Hr, Wb_r], writes=[ps_r[b]])
                    Sx.op("dve", lambda e, b=b, u=u: e.tensor_copy(out=wist[:, u, :], in_=ps[:, b, 0:8]), reads=[ps_r[b]], writes=[wist_r])
                Sx.op("sp", lambda e, go=go: e.dma_start(out=wiD.rearrange("(n u p) c -> n p u c", u=4, p=128)[go], in_=wist[:]), reads=[wist_r], writes=[dram_res["wiD"]], dma_tag="wist")

            xTo_v = xTo.rearrange("(c p) t -> p c t", p=128)
            passes = [("k", g) for g in range(NG)] + [("q", g) for g in range(NOG)]
            for pi, (kindp, g) in enumerate(passes):
                sl = pi % 2
                t0 = g * 512
                is_own = kindp == "q"
                go = g
                srcv = xTo_v if is_own else xT_v
                Sx.op("sp", lambda e, sl=sl, t0=t0, srcv=srcv: e.dma_start(out=xs[sl][:], in_=srcv[:, :, t0:t0 + 512]), writes=[xs_r[sl]], dma_tag="xs0")
                if not is_own:
                    Sx.op("sp", lambda e, t0=t0: e.dma_start(out=tb[0][0:32, :], in_=ck_d[:, t0:t0 + 512]), writes=[tb_r[0]], dma_tag="tb0")
                    Sx.op("sp", lambda e, t0=t0: e.dma_start(out=tb[1][0:32, :], in_=sk_d[:, t0:t0 + 512]), writes=[tb_r[1]], dma_tag="tb1")
                else:
                    Sx.op("sp", lambda e, go=go: e.dma_start(out=tb[2][:], in_=cq_d[:, go * 512:go * 512 + 512]), writes=[tb_r[2]], dma_tag="tb2")
                    Sx.op("sp", lambda e, go=go: e.dma_start(out=tb[3][:], in_=sq_d[:, go * 512:go * 512 + 512]), writes=[tb_r[3]], dma_tag="tb3")
                norm_stats([xs[sl][:, c, :] for c in range(8)], 8, sq, sq_r, xs_r[sl], B_SS, float(D), rt, rt_r, R1, R1_r)
                for c in range(8):
                    eng = "dve" if c % 2 == 0 else "pool"
                    Sx.op(eng, lambda e, c=c, sl=sl: e.tensor_tensor(out=hT[sl][:, c, :], in0=xs[sl][:, c, :], in1=R1[:], op=ALU.mult), reads=[xs_r[sl], R1_r], writes=[hT_r[sl]])
                H, Hr = hT[sl], hT_r[sl]
                if is_own:
                    q_side(H, Hr, go)
                    continue
                for c2 in range(2):
                    b = fm_proj(H, Hr, C_CKV + c2 * 128, 128, 8, Wb, Wb_r)
                    evac(b, 128, raw[:, c2, :], raw_r)
                norm_stats([raw[:, c, :] for c in range(2)], 2, sq2, sq2_r, raw_r, B_SS2, 256.0, rt2, rt2_r, R2, R2_r)
                for c2 in range(2):
                    Sx.op("pool", lambda e, c2=c2: e.tensor_tensor(out=ckvn[:, c2, :], in0=raw[:, c2, :], in1=R2[:], op=ALU.mult), reads=[raw_r, R2_r], writes=[ckvn_r])
                for h in range(8):
                    b = fm_proj(ckvn, ckvn_r, h * 64, 64, 2, Wk, W2_r)
                    evac(b, 64, knst[:, h % 4, :], knst_r)
                    if h % 4 == 3:
                        Sx.op("sp", lambda e, t0=t0, h=h: e.dma_start(out=knT.rearrange("h p t -> p h t")[:, h - 3:h + 1, t0:t0 + 512], in_=knst[:]), reads=[knst_r], writes=[dram_res["knT"]], dma_tag="knst")
                for u in range(4):
                    b = ntm()
                    for c in range(2):
                        Sx.op("pe", lambda e, c=c, b=b, u=u: e.matmul(ps[:, b, :], lhsT=ckvn[:, c, u * 128:(u + 1) * 128], rhs=Wv[:, c, :], start=(c == 0), stop=(c == 1)),
                              reads=[ckvn_r, W2_r], writes=[ps_r[b]])
                    Sx.op("act", lambda e, b=b, u=u: e.activation(out=vst[0][:, u, :, 0:64], in_=ps[:, b, :].rearrange("p (h d) -> p h d", h=8), func=AF.Copy), reads=[ps_r[b]], writes=[vst_r[0]])
                Sx.op("sp", lambda e, g=g: e.dma_start(out=vA.rearrange("(n u p) c -> n p u c", u=4, p=128)[g], in_=vst[0][:].rearrange("p u h d -> p u (h d)")), reads=[vst_r[0]], writes=[dram_res["vA"]], dma_tag="vst0")
                bA = fm_proj(H, Hr, C_KROPE, 32, 8, Wb, Wb_r)
                bB = fm_proj(H, Hr, C_KROT, 32, 8, Wb, Wb_r)
                rope_combine(bA, bB, 32, tb[0], tb[1], tb_r[0], tb_r[1], kpst[:], kpst_r)
                Sx.op("sp", lambda e, t0=t0: e.dma_start(out=kpeT[:, t0:t0 + 512], in_=kpst[:]), reads=[kpst_r], writes=[dram_res["kpeT"]], dma_tag="kpst")
                for c4 in range(4):
                    b = fm_proj(H, Hr, C_KB + c4 * 128, 128, 8, Wb, Wb_r)
                    evac(b, 128, kbst[:, c4, :], kbst_r)
                Sx.op("sp", lambda e, t0=t0: e.dma_start(out=kbT.rearrange("a p t -> p a t")[:, :, t0:t0 + 512], in_=kbst[:]), reads=[kbst_r], writes=[dram_res["kbT"]], dma_tag="kbst")
                b = fm_proj(H, Hr, C_KI, 32, 8, Wb, Wb_r)
                evac(b, 32, kist[:], kist_r)
                Sx.op("sp", lambda e, t0=t0: e.dma_start(out=kiT[:, t0:t0 + 512], in_=kist[:]), reads=[kist_r], writes=[dram_res["kiT"]], dma_tag="kist")
                for u in range(4):
                    b = ntm()
                    for c in range(8):
                        Sx.op("pe", lambda e, c=c, b=b, u=u: e.matmul(ps[:, b, :], lhsT=H[:, c, u * 128:(u + 1) * 128], rhs=Wb[:, c, C_VB:C_VB + 512], start=(c == 0), stop=(c == 7)),
                              reads=[Hr, Wb_r], writes=[ps_r[b]])
                    Sx.op("act", lambda e, b=b, u=u: e.activation(out=vst[1][:, u, :, 0:64], in_=ps[:, b, :].rearrange("p (h d) -> p h d", h=8), func=AF.Copy), reads=[ps_r[b]], writes=[vst_r[1]])
                Sx.op("sp", lambda e, g=g: e.dma_start(out=vB.rearrange("(n u p) c -> n p u c", u=4, p=128)[g], in_=vst[1][:].rearrange("p u h d -> p u (h d)")), reads=[vst_r[1]], writes=[dram_res["vB"]], dma_tag="vst1")
            Sx.emit(allres)

        def load_vaug(Sx, Vt, V_r, src):
            sv = src.rearrange("(j p) c -> p j c", p=128)
            for j0 in range(0, NT, 8):
                Sx.op("sp", lambda e, j0=j0: e.dma_start(out=Vt[:, j0:j0 + 8, :], in_=sv[:, j0:j0 + 8, :]), writes=[V_r[j0 // 8]], dma_tag="V%d" % (j0 // 8))

        with ExitStack() as st:
            sb = lambda n, s, d: st.enter_context(nc.sbuf_tensor("sb_" + n, s, d))
            Sx = Sched(ctx)
            Kt = [sb("Kt%d" % i, [96, S], BF16) for i in range(2)]
            Kt_r = [R() for _ in range(2)]
            qT = [sb("qT%d" % i, [96, NO], BF16) for i in range(2)]
            qT_r = [R() for _ in range(2)]
            Vt = sb("Vt", [128, NT, 520], BF16)
            V_r = [R() for _ in range(NT // 8)]
            Pt = [sb("Pt%d" % i, [128, 512], BF16) for i in range(3)]
            Pt_r = [R() for _ in range(3)]
            mT_f = sb("mT_f", [128, 256], F32)
            mT = sb("mT", [128, 256], BF16)
            mT_r = R()
            ya = [sb("ya%d" % i, [128, 8, 512], BF16) for i in range(1)]
            ya_r = [R()]
            rec = sb("rec", [128, 1], F32)
            rec_r = R()
            ps = st.enter_context(nc.psum_tensor("ps2", [128, 8, 512], F32))
            ps_r = [R() for _ in range(8)]
            Sx.op("sp", lambda e: e.dma_start(out=mT_f[:], in_=maskA_d), writes=[mT_r], dma_tag="mT")
            Sx.op("dve", lambda e: e.tensor_copy(out=mT[:], in_=mT_f[:]), reads=[mT_r], writes=[mT_r])
            load_vaug(Sx, Vt, V_r, vA)
            sc = 96.0 ** -0.5
            gi = 0
            oi = 0
            for h in range(8):
                sl = h % 2
                for hs in range(0, S, HS):
                    Sx.op("sp", lambda e, sl=sl, h=h, hs=hs: e.dma_start(out=Kt[sl][0:64, hs:hs + HS], in_=knT[h, :, hs:hs + HS]), writes=[Kt_r[sl]], dma_tag="Kt%da" % sl)
                    Sx.op("sp", lambda e, sl=sl, hs=hs: e.dma_start(out=Kt[sl][64:96, hs:hs + HS], in_=kpeT[:, hs:hs + HS]), writes=[Kt_r[sl]], dma_tag="Kt%db" % sl)
                Sx.op("sp", lambda e, sl=sl, h=h: e.dma_start(out=qT[sl][:], in_=qaT[h]), writes=[qT_r[sl]], dma_tag="qT%d" % sl)
                for qi in range(NOT_):
                    i = 2 * qi + 1
                    ob = 6 + (oi % 2)
                    oi += 1
                    for j0 in range(0, i + 1, 4):
                        js = list(range(j0, min(i + 1, j0 + 4)))
                        sbk = gi % 6
                        psl = gi % 3
                        gi += 1
                        for jj, j in enumerate(js):
                            Sx.op("pe", lambda e, jj=jj, j=j, sl=sl, sbk=sbk, qi=qi: e.matmul(ps[:, sbk, jj * 128:(jj + 1) * 128], lhsT=Kt[sl][:, j * 128:(j + 1) * 128], rhs=qT[sl][:, qi * 128:(qi + 1) * 128], start=True, stop=True),
                                  reads=[Kt_r[sl], qT_r[sl]], writes=[ps_r[sbk]])
                        n = len(js) * 128
                        Sx.op("act", lambda e, sbk=sbk, psl=psl, n=n: e.activation(out=Pt[psl][:, 0:n], in_=ps[:, sbk, 0:n], func=AF.Exp, scale=sc), reads=[ps_r[sbk]], writes=[Pt_r[psl]])
                        for a in range(2):
                            if (i - 1 + a) in js:
                                jj = js.index(i - 1 + a)
                                Sx.op("dve", lambda e, psl=psl, jj=jj, a=a: e.tensor_tensor(out=Pt[psl][:, jj * 128:(jj + 1) * 128], in0=Pt[psl][:, jj * 128:(jj + 1) * 128], in1=mT[:, a * 128:(a + 1) * 128], op=ALU.mult),
                                      reads=[Pt_r[psl], mT_r], writes=[Pt_r[psl]])
                        for jj, j in enumerate(js):
                            Sx.op("pe", lambda e, jj=jj, j=j, psl=psl, ob=ob, h=h, i=i: e.matmul(ps[:, ob, 0:65], lhsT=Pt[psl][:, jj * 128:(jj + 1) * 128], rhs=Vt[:, j, h * 65:(h + 1) * 65], start=(j == 0), stop=(j == i)),
                                  reads=[Pt_r[psl], V_r[j // 8]], writes=[ps_r[ob]])
                    Sx.op("dve", lambda e, ob=ob: e.reciprocal(out=rec[:], in_=ps[:, ob, 64:65]), reads=[ps_r[ob]], writes=[rec_r])
                    Sx.op("dve", lambda e, ob=ob, qi=qi, h=h: e.tensor_scalar(out=ya[0][:, qi % 8, 0:64], in0=ps[:, ob, 0:64], scalar1=rec[:], scalar2=None, op0=ALU.mult),
                          reads=[ps_r[ob], rec_r], writes=[ya_r[0]])
                    if qi % 8 == 7 or qi == NOT_ - 1:
                        q8 = (qi // 8) * 8
                        nq = qi - q8 + 1
                        Sx.op("sp", lambda e, q8=q8, nq=nq, h=h: e.dma_start(out=yaD.rearrange("(q p) c -> p q c", p=128)[:, q8:q8 + nq, h * 64:(h + 1) * 64], in_=ya[0][:, 0:nq, 0:64]), reads=[ya_r[0]], writes=[dram_res["yaD"]], dma_tag="ya")
            Sx.emit(allres)

        with ExitStack() as st:
            sb = lambda n, s, d: st.enter_context(nc.sbuf_tensor("sb_" + n, s, d))
            Sx = Sched(ctx)
            Kb = sb("Kb", [128, 4, S], BF16)
            Kb_r = R()
            Vt = sb("Vt3", [128, NT, 520], BF16)
            V_r = [R() for _ in range(NT // 8)]
            Ki = sb("Ki", [64, S // 2], BF16)
            Ki_r = R()
            scs = sb("scs", [128, S], BF16)
            scs_r = R()
            nm = sb("nm", [128, S], BF16)
            nm_r = R()
            Pt = [sb("Pt3%d" % i, [128, 512], BF16) for i in range(3)]
            Pt_r = [R() for _ in range(3)]
            Rl = [sb("Rl%d" % i, [128, 2, 512], BF16) for i in range(2)]
            Rl_r = [R() for _ in range(2)]
            id_f = sb("id_f", [128, 128], F32)
            idb = sb("idb", [128, 128], BF16)
            id_r = R()
            bT_f = sb("bT_f", [128, 768], F32)
            bT = sb("bT", [128, 8, 3, 128], BF16)
            adm_f = sb("adm_f", [128, 256], F32)
            adm = sb("adm", [128, 256], BF16)
            adm_r = R()
            bT_r = R()
            rb15 = sb("rb15", [128, 8], F32)
            rb_r = R()
            pw = sb("pw", [128, 2 * (NB + 1)], F32)
            pw_r = R()
            qb = [sb("qb%d" % i, [128, 4, 128], BF16) for i in range(2)]
            qb_r = [R() for _ in range(2)]
            qit = [sb("qit%d" % i, [64, 8, 128], BF16) for i in range(2)]
            qit_r = [R() for _ in range(2)]
            wit = [sb("wit%d" % i, [128, 8], F32) for i in range(2)]
            wit_r = [R() for _ in range(2)]
            Dg = sb("Dg", [128, 8, 128], BF16)
            Dg_r = R()
            yb = [sb("yb%d" % i, [128, 512], BF16) for i in range(2)]
            yb_r = [R() for _ in range(2)]
            rec = sb("rec3", [128, 1], F32)
            rec_r = R()
            sm = sb("sm", [128, 8], F32)
            sm_r = R()
            stp = sb("stp", [128, 2 * (NB + 1)], F32)
            stp_r = R()
            ps = st.enter_context(nc.psum_tensor("ps3", [128, 8, 512], F32))
            ps_r = [R() for _ in range(8)]
            Sx.op("sp", lambda e: e.dma_start(out=id_f[:], in_=ident_d), writes=[id_r], dma_tag="id")
            Sx.op("dve", lambda e: e.tensor_copy(out=idb[:], in_=id_f[:]), reads=[id_r], writes=[id_r])
            for q4 in range(4):
                Sx.op("sp", lambda e, q4=q4: e.dma_start(out=bT_f[:], in_=biasT_d[:, q4 * 768:(q4 + 1) * 768]), writes=[bT_r], dma_tag="bT")
                Sx.op("dve", lambda e, q4=q4: e.tensor_copy(out=bT[:].rearrange("p h k t -> p (h k t)")[:, q4 * 768:(q4 + 1) * 768], in_=bT_f[:]), reads=[bT_r], writes=[bT_r])
            Sx.op("sp", lambda e: e.dma_start(out=rb15[:], in_=rb15_d), writes=[rb_r], dma_tag="rb")
            Sx.op("sp", lambda e: e.dma_start(out=adm_f[:], in_=adm_d), writes=[adm_r], dma_tag="adm")
            Sx.op("dve", lambda e: e.tensor_copy(out=adm[:], in_=adm_f[:]), reads=[adm_r], writes=[adm_r])
            Sx.op("sp", lambda e: e.dma_start(out=pw[:], in_=pw_d), writes=[pw_r], dma_tag="pw")
            for a4 in range(4):
                for hs in range(0, S, HS):
                    Sx.op("sp", lambda e, a4=a4, hs=hs: e.dma_start(out=Kb[:, a4, hs:hs + HS], in_=kbT[a4, :, hs:hs + HS]), writes=[Kb_r], dma_tag="Kb")
            Sx.op("sp", lambda e: e.dma_start(out=Ki[0:32, :], in_=kiT[:, 0:S // 2]), writes=[Ki_r], dma_tag="Ki")
            Sx.op("sp", lambda e: e.dma_start(out=Ki[32:64, :], in_=kiT[:, S // 2:S]), writes=[Ki_r], dma_tag="Ki2")
            load_vaug(Sx, Vt, V_r, vB)
            gi = 0
            oi = 0
            xi = 0
            for qi in range(NOT_):
                i = 2 * qi + 1
                sl = qi % 2
                q0 = qi * 128
                nk = (i + 1) * 128
                Sx.op("sp", lambda e, sl=sl, q0=q0: e.dma_start(out=qb[sl][:], in_=qbT.rearrange("a p t -> p a t")[:, :, q0:q0 + 128]), writes=[qb_r[sl]], dma_tag="qb%d" % sl)
                Sx.op("sp", lambda e, sl=sl, q0=q0: e.dma_start(out=qit[sl][0:32], in_=qiT.rearrange("h d t -> d h t")[:, :, q0:q0 + 128]), writes=[qit_r[sl]], dma_tag="qit%d" % sl)
                Sx.op("sp", lambda e, sl=sl, q0=q0: e.dma_start(out=qit[sl][32:64], in_=qiT.rearrange("h d t -> d h t")[:, :, q0:q0 + 128]), writes=[qit_r[sl]], dma_tag="qjt%d" % sl)
                Sx.op("sp", lambda e, sl=sl, q0=q0: e.dma_start(out=wit[sl][:], in_=wiD[q0:q0 + 128, :]), writes=[wit_r[sl]], dma_tag="wit%d" % sl)
                for h in range(8):
                    eng = "dve" if h % 2 == 0 else "pool"
                    Sx.op(eng, lambda e, h=h, sl=sl: e.tensor_scalar(out=Dg[:, h, :], in0=id_f[:], scalar1=wit[sl][:, h:h + 1], scalar2=None, op0=ALU.mult), reads=[id_r, wit_r[sl]], writes=[Dg_r])
                for k0 in range(0, nk, 512):
                    n = min(512, nk - k0)
                    for hp in range(4):
                        xb = (xi % 2) * 2
                        rs = xi % 2
                        xi += 1
                        for hh in range(2):
                            h = hp * 2 + hh
                            kp = 0 if k0 < S // 2 else 32
                            kc = k0 - (0 if k0 < S // 2 else S // 2)
                            Sx.op("pe", lambda e, xb=xb, hh=hh, h=h, sl=sl, kc=kc, kp=kp, n=n: e.matmul(ps[:, xb + hh, 0:n], lhsT=qit[sl][kp:kp + 32, h, :], rhs=Ki[kp:kp + 32, kc:kc + n], start=True, stop=True),
                                  reads=[qit_r[sl], Ki_r], writes=[ps_r[xb + hh]])
                        Sx.op("act", lambda e, xb=xb, rs=rs, n=n: e.activation(out=Rl[rs][:, :, 0:n], in_=ps[:, xb:xb + 2, 0:n], func=AF.Relu), reads=[ps_r[xb], ps_r[xb + 1]], writes=[Rl_r[rs]])
                        for hh in range(2):
                            h = hp * 2 + hh
                            Sx.op("pe", lambda e, rs=rs, hh=hh, h=h, n=n: e.matmul(ps[:, 4, 0:n], lhsT=Dg[:, h, :], rhs=Rl[rs][:, hh, 0:n], start=(h == 0), stop=(h == 7)),
                                  reads=[Dg_r, Rl_r[rs]], writes=[ps_r[4]])
                    Sx.op("dve", lambda e, k0=k0, n=n: e.tensor_copy(out=scs[:, k0:k0 + n], in_=ps[:, 4, 0:n]), reads=[ps_r[4]], writes=[scs_r])
                if qi >= 1:
                    Sx.op("dve", lambda e, nk=nk: e.tensor_scalar(out=nm[:, 0:nk], in0=scs[:, 0:nk], scalar1=1.0, scalar2=-1e30, op0=ALU.mult, op1=ALU.max, accum_out=sm[:, 0:1]), reads=[scs_r], writes=[nm_r, sm_r])
                    Sx.op("dve", lambda e, nk=nk: e.tensor_scalar(out=nm[:, 0:nk], in0=scs[:, 0:nk], scalar1=-1.0, scalar2=-1e30, op0=ALU.mult, op1=ALU.max, accum_out=sm[:, 1:2]), reads=[scs_r], writes=[nm_r, sm_r])
                    Sx.op("dve", lambda e: e.tensor_tensor(out=sm[:, 2:3], in0=sm[:, 0:1], in1=sm[:, 1:2], op=ALU.subtract), reads=[sm_r], writes=[sm_r])
                    Sx.op("dve", lambda e: e.tensor_tensor(out=sm[:, 3:4], in0=sm[:, 0:1], in1=sm[:, 1:2], op=ALU.add), reads=[sm_r], writes=[sm_r])
                    Sx.op("dve", lambda e: e.tensor_scalar(out=sm[:, 2:4], in0=sm[:, 2:4], scalar1=0.5, scalar2=None, op0=ALU.mult), reads=[sm_r], writes=[sm_r])
                    Sx.op("dve", lambda e: e.tensor_scalar(out=stp[:], in0=pw[:], scalar1=sm[:, 3:4], scalar2=None, op0=ALU.mult), reads=[sm_r, pw_r], writes=[stp_r])
                Sx.op("pool", lambda e, nk=nk: e.tensor_tensor(out=scs[:, nk - 256:nk], in0=scs[:, nk - 256:nk], in1=adm[:], op=ALU.add), reads=[sm_r, adm_r, scs_r], writes=[scs_r])
                if qi >= 1:
                    for kk in range(1, NB + 1):
                        Sx.op("dve", lambda e, nk=nk: e.tensor_scalar(out=nm[:, 0:nk], in0=scs[:, 0:nk], scalar1=sm[:, 2:3], scalar2=0.0, op0=ALU.is_ge, op1=ALU.add, accum_out=sm[:, 4:5]), reads=[scs_r, sm_r], writes=[nm_r, sm_r])
                        Sx.op("dve", lambda e: e.tensor_scalar(out=sm[:, 5:6], in0=sm[:, 4:5], scalar1=TOPK, scalar2=0.5, op0=ALU.is_ge, op1=ALU.subtract), reads=[sm_r], writes=[sm_r])
                        Sx.op("dve", lambda e, kk=kk: e.scalar_tensor_tensor(out=sm[:, 2:3], in0=sm[:, 5:6], scalar=stp[:, NB + 1 + kk:NB + 2 + kk], in1=sm[:, 2:3], op0=ALU.mult, op1=ALU.add), reads=[sm_r, stp_r], writes=[sm_r])
                    Sx.op("dve", lambda e: e.tensor_tensor(out=sm[:, 2:3], in0=sm[:, 2:3], in1=stp[:, NB:NB + 1], op=ALU.subtract), reads=[sm_r, stp_r], writes=[sm_r])
                else:
                    Sx.op("dve", lambda e: e.memset(sm[:, 2:3], -1e29), writes=[sm_r])
                Sx.op("dve", lambda e, nk=nk: e.tensor_scalar(out=nm[:, 0:nk], in0=scs[:, 0:nk], scalar1=sm[:, 2:3], scalar2=NEG, op0=ALU.is_lt, op1=ALU.mult), reads=[scs_r, sm_r], writes=[nm_r])
                near0 = max(0, i - 2)
                groups = [(list(range(j0, min(near0, j0 + 4))), False) for j0 in range(0, near0, 4)] + [(list(range(near0, i + 1)), True)]
                for h in range(8):
                    ob = 6 + (oi % 2)
                    oi += 1
                    hp, po = h // 2, 64 * (h % 2)
                    for (js, near) in groups:
                        sbk = [0, 1, 2, 3, 5][gi % 5]
                        psl = gi % 3
                        gi += 1
                        for jj, j in enumerate(js):
                            cs = slice(jj * 128, (jj + 1) * 128)
                            Sx.op("pe", lambda e, cs=cs, j=j, sbk=sbk, hp=hp, po=po, sl=sl: e.matmul(ps[:, sbk, cs], lhsT=Kb[po:po + 64, hp, j * 128:(j + 1) * 128], rhs=qb[sl][po:po + 64, hp, :], start=True, stop=False),
                                  reads=[Kb_r, qb_r[sl]], writes=[ps_r[sbk]])
                            Sx.op("pe", lambda e, cs=cs, j=j, sbk=sbk, near=near: e.matmul(ps[:, sbk, cs], lhsT=nm[:, j * 128:(j + 1) * 128], rhs=idb[:], start=False, stop=(not near)),
                                  reads=[nm_r, id_r], writes=[ps_r[sbk]])
                            if near:
                                kind = j - (i - 2)
                                Sx.op("pe", lambda e, cs=cs, sbk=sbk, h=h, kind=kind: e.matmul(ps[:, sbk, cs], lhsT=idb[:], rhs=bT[:, h, kind, :], start=False, stop=True),
                                      reads=[bT_r, id_r], writes=[ps_r[sbk]])
                        n = len(js) * 128
                        if near:
                            Sx.op("act", lambda e, sbk=sbk, psl=psl, n=n: e.activation(out=Pt[psl][:, 0:n], in_=ps[:, sbk, 0:n], func=AF.Exp), reads=[ps_r[sbk]], writes=[Pt_r[psl]])
                        else:
                            Sx.op("act", lambda e, sbk=sbk, psl=psl, n=n, h=h: e.activation(out=Pt[psl][:, 0:n], in_=ps[:, sbk, 0:n], func=AF.Exp, bias=rb15[:, h:h + 1]), reads=[ps_r[sbk], rb_r], writes=[Pt_r[psl]])
                        for jj, j in enumerate(js):
                            Sx.op("pe", lambda e, jj=jj, j=j, psl=psl, ob=ob, h=h, i=i: e.matmul(ps[:, ob, 0:65], lhsT=Pt[psl][:, jj * 128:(jj + 1) * 128], rhs=Vt[:, j, h * 65:(h + 1) * 65], start=(j == 0), stop=(j == i)),
                                  reads=[Pt_r[psl], V_r[j // 8]], writes=[ps_r[ob]])
                    Sx.op("dve", lambda e, ob=ob: e.reciprocal(out=rec[:], in_=ps[:, ob, 64:65]), reads=[ps_r[ob]], writes=[rec_r])
                    Sx.op("dve", lambda e, ob=ob, sl=sl, h=h: e.tensor_scalar(out=yb[sl][:, h * 64:(h + 1) * 64], in0=ps[:, ob, 0:64], scalar1=rec[:], scalar2=None, op0=ALU.mult),
                          reads=[ps_r[ob], rec_r], writes=[yb_r[sl]])
                Sx.op("sp", lambda e, sl=sl, q0=q0: e.dma_start(out=ybD[q0:q0 + 128, :], in_=yb[sl][:]), reads=[yb_r[sl]], writes=[dram_res["ybD"]], dma_tag="yb%d" % sl)
            Sx.emit(allres)

        with ExitStack() as st:
            sb = lambda n, s, d: st.enter_context(nc.sbuf_tensor("sb_" + n, s, d))
            Sx = Sched(ctx)
            wst = [sb("wst4%d" % i, [128, 1024], F32) for i in range(2)]
            wst_r = [R() for _ in range(2)]
            Woa = sb("Woa", [128, 4, D], BF16)
            Wob = sb("Wob", [128, 4, D], BF16)
            Wo = sb("Wo", [128, 8, D], BF16)
            W_r = R()
            fg = sb("fg", [128, D], F32)
            fg_r = R()
            id_f = sb("id_f4", [128, 128], F32)
            idb = sb("idb4", [128, 128], BF16)
            id_r = R()
            yat = [sb("yat%d" % i, [128, 512], BF16) for i in range(2)]
            ybt = [sb("ybt%d" % i, [128, 512], BF16) for i in range(2)]
            zgt = [sb("zgt%d" % i, [128, 3072], BF16) for i in range(2)]
            xt = [sb("xt%d" % i, [128, D], F32) for i in range(2)]
            in_r = [R() for _ in range(2)]
            gab = sb("gab", [128, 2, 512], BF16)
            gab_r = R()
            gT = sb("gT", [128, 8, 128], BF16)
            gT_r = R()
            m1 = sb("m1", [128, D], F32)
            m1_r = R()
            m2 = sb("m2", [128, D], F32)
            m2_r = R()
            mg = sb("mg", [128, D], BF16)
            mg_r = R()
            mgT = sb("mgT", [128, 8, 128], BF16)
            mgT_r = R()
            xo_t = sb("xo_t", [128, D], F32)
            xo_r = R()
            junk = sb("junk4", [128, D], F32)
            junk_r = R()
            ssq = sb("ssq", [128, 2], F32)
            ssq_r = R()
            ot = [sb("ot%d" % i, [128, D], F32) for i in range(2)]
            ot_r = [R() for _ in range(2)]
            ps = st.enter_context(nc.psum_tensor("ps4", [128, 6, 512], F32))
            ps_r = [R() for _ in range(6)]
            pT = st.enter_context(nc.psum_tensor("pT4", [128, 2, 1024], BF16))
            pT_r = [R() for _ in range(2)]
            k = 0
            for (dst, src, nch) in [(Woa, w_oa, 4), (Wob, w_ob, 4), (Wo, w_out, 8)]:
                v = src.rearrange("(c p) n -> p c n", p=128)
                for c in range(nch):
                    sl = k % 2
                    Sx.op("sp", lambda e, c=c, sl=sl, v=v: e.dma_start(out=wst[sl][:], in_=v[:, c, :]), writes=[wst_r[sl]], dma_tag="w4%d" % sl)
                    Sx.op("dve" if k % 2 == 0 else "pool", lambda e, c=c, sl=sl, dst=dst: e.tensor_copy(out=dst[:, c, :], in_=wst[sl][:]), reads=[wst_r[sl]], writes=[W_r])
                    k += 1
            Sx.op("sp", lambda e: e.dma_start(out=fg[:], in_=fg_b), writes=[fg_r], dma_tag="fg")
            Sx.op("sp", lambda e: e.dma_start(out=id_f[:], in_=ident_d), writes=[id_r], dma_tag="id4")
            Sx.op("dve", lambda e: e.tensor_copy(out=idb[:], in_=id_f[:]), reads=[id_r], writes=[id_r])
            for qi in range(NOT_):
                sl = qi % 2
                q0 = qi * 128
                Sx.op("sp", lambda e, sl=sl, q0=q0: e.dma_start(out=yat[sl][:], in_=yaD[q0:q0 + 128, :]), reads=[dram_res["yaD"]], writes=[in_r[sl]], dma_tag="i4a%d" % sl)
                Sx.op("sp", lambda e, sl=sl, q0=q0: e.dma_start(out=ybt[sl][:], in_=ybD[q0:q0 + 128, :]), reads=[dram_res["ybD"]], writes=[in_r[sl]], dma_tag="i4b%d" % sl)
                Sx.op("sp", lambda e, sl=sl, q0=q0: e.dma_start(out=zgt[sl][:], in_=zg[q0:q0 + 128, :]), reads=[dram_res["zg"]], writes=[in_r[sl]], dma_tag="i4c%d" % sl)
                Sx.op("sp", lambda e, sl=sl, q0=q0: e.dma_start(out=xt[sl][:], in_=xo[q0:q0 + 128, :]), writes=[in_r[sl]], dma_tag="i4d%d" % sl)
                Sx.op("dve", lambda e, sl=sl: e.tensor_tensor(out=gab[:, 0, :], in0=yat[sl][:], in1=zgt[sl][:, 0:512], op=ALU.mult), reads=[in_r[sl]], writes=[gab_r])
                Sx.op("pool", lambda e, sl=sl: e.tensor_tensor(out=gab[:, 1, :], in0=ybt[sl][:], in1=zgt[sl][:, 512:1024], op=ALU.mult), reads=[in_r[sl]], writes=[gab_r])
                for m in range(8):
                    Sx.op("pe", lambda e, m=m: e.transpose(out=pT[:, 0, m * 128:(m + 1) * 128], in_=gab[:, m // 4, (m % 4) * 128:(m % 4 + 1) * 128], identity=idb[:]), reads=[gab_r, id_r], writes=[pT_r[0]])
                Sx.op("act", lambda e: e.activation(out=gT[:].rearrange("p m t -> p (m t)"), in_=pT[:, 0, :], func=AF.Copy), reads=[pT_r[0]], writes=[gT_r])
                for ab in range(2):
                    Wt = Woa if ab == 0 else Wob
                    for hf in range(2):
                        b = ab * 2 + hf
                        for c in range(4):
                            Sx.op("pe", lambda e, b=b, c=c, ab=ab, hf=hf, Wt=Wt: e.matmul(ps[:, b, :], lhsT=gT[:, ab * 4 + c, :], rhs=Wt[:, c, hf * 512:(hf + 1) * 512], start=(c == 0), stop=(c == 3)),
                                  reads=[gT_r, W_r], writes=[ps_r[b]])
                for hf in range(2):
                    Sx.op("dve", lambda e, hf=hf, sl=sl: e.tensor_tensor(out=m1[:, hf * 512:(hf + 1) * 512], in0=ps[:, hf, :], in1=zgt[sl][:, 1024 + hf * 512:1536 + hf * 512], op=ALU.mult), reads=[ps_r[hf], in_r[sl]], writes=[m1_r])
                    Sx.op("dve", lambda e, hf=hf, sl=sl: e.tensor_tensor(out=m2[:, hf * 512:(hf + 1) * 512], in0=ps[:, 2 + hf, :], in1=zgt[sl][:, 2048 + hf * 512:2560 + hf * 512], op=ALU.mult), reads=[ps_r[2 + hf], in_r[sl]], writes=[m2_r])
                Sx.op("pool", lambda e: e.tensor_tensor(out=mg[:], in0=m1[:], in1=m2[:], op=ALU.add), reads=[m1_r, m2_r], writes=[mg_r])
                for m in range(8):
                    Sx.op("pe", lambda e, m=m: e.transpose(out=pT[:, 1, m * 128:(m + 1) * 128], in_=mg[:, m * 128:(m + 1) * 128], identity=idb[:]), reads=[mg_r, id_r], writes=[pT_r[1]])
                Sx.op("act", lambda e: e.activation(out=mgT[:].rearrange("p m t -> p (m t)"), in_=pT[:, 1, :], func=AF.Copy), reads=[pT_r[1]], writes=[mgT_r])
                for hf in range(2):
                    b = 4 + hf
                    for c in range(8):
                        Sx.op("pe", lambda e, b=b, c=c, hf=hf: e.matmul(ps[:, b, :], lhsT=mgT[:, c, :], rhs=Wo[:, c, hf * 512:(hf + 1) * 512], start=(c == 0), stop=(c == 7)),
                              reads=[mgT_r, W_r], writes=[ps_r[b]])
                    Sx.op("dve", lambda e, b=b, hf=hf, sl=sl: e.tensor_tensor(out=xo_t[:, hf * 512:(hf + 1) * 512], in0=ps[:, b, :], in1=xt[sl][:, hf * 512:(hf + 1) * 512], op=ALU.add), reads=[ps_r[b], in_r[sl]], writes=[xo_r])
                Sx.op("act", lambda e: e.activation(out=junk[:], in_=xo_t[:], func=AF.Square, accum_out=ssq[:, 0:1]), reads=[xo_r], writes=[junk_r, ssq_r])
                Sx.op("act", lambda e: e.activation(out=ssq[:, 1:2], in_=ssq[:, 0:1], func=AF.Sqrt, bias=EPS, scale=1.0 / D), reads=[ssq_r], writes=[ssq_r])
                Sx.op("dve", lambda e: e.reciprocal(out=ssq[:, 1:2], in_=ssq[:, 1:2]), reads=[ssq_r], writes=[ssq_r])
                Sx.op("dve", lambda e, sl=sl: e.scalar_tensor_tensor(out=ot[sl][:], in0=xo_t[:], scalar=ssq[:, 1:2], in1=fg[:], op0=ALU.mult, op1=ALU.mult), reads=[xo_r, ssq_r, fg_r], writes=[ot_r[sl]])
                Sx.op("sp", lambda e, sl=sl, q0=q0: e.dma_start(out=out_d[q0:q0 + 128, :], in_=ot[sl][:]), reads=[ot_r[sl]], writes=[dram_res["out"]], dma_tag="o4%d" % sl)
            Sx.emit(allres)
    return nc


def _t5_bucket(rel):
    nb, max_exact = 16, 8
    ret = (rel > 0).astype(np.int32) * nb
    n = np.abs(rel)
    nf = np.maximum(n, 1).astype(np.float32)
    large = max_exact + (np.log(nf / np.float32(max_exact)) / np.float32(np.log(128 / max_exact)) * np.float32(nb - max_exact)).astype(np.int32)
    large = np.minimum(large, nb - 1)
    return ret + np.where(n < max_exact, n, large)


_NC_CACHE = {}


def kernel(x, norm_g, w_in, g_q_lat, w_uq, g_kv_lat, w_ukv, w_o_a, w_o_b, w_out, rel_bias, final_g):
    x = np.asarray(x, np.float32)
    B, S, _ = x.shape
    f = lambda a: np.ascontiguousarray(np.asarray(a, np.float32))
    norm_g, w_in, g_q_lat, w_uq, g_kv_lat, w_ukv = f(norm_g)[0], f(w_in)[0], f(g_q_lat)[0], f(w_uq)[0], f(g_kv_lat)[0], f(w_ukv)[0]
    w_o_a, w_o_b, w_out, rel_bias, final_g = f(w_o_a)[0], f(w_o_b)[0], f(w_out)[0], f(rel_bias), f(final_g)
    NG = S // 512
    perm = np.concatenate([np.arange(16, 32), np.arange(0, 16)])
    w_in_ext = np.concatenate([w_in, w_in[:, C_KROPE:C_KROPE + 32][:, perm]], axis=1)
    w_uqb = np.zeros_like(w_uq)
    for h in range(8):
        w_uqb[:, h * 96 + 64:h * 96 + 96] = w_uq[:, h * 96 + 64:h * 96 + 96][:, perm]
    w3 = w_ukv.reshape(256, 8, 128)
    w_uk = np.ascontiguousarray(w3[:, :, :64].reshape(256, 512))
    w_uv = np.ascontiguousarray(w3[:, :, 64:].reshape(256, 512))
    lay = lambda g, c: np.ascontiguousarray(g.reshape(c, 128).T)
    half = 16
    freqs = (np.float32(10000.0) ** (-np.arange(half, dtype=np.float32) / np.float32(half))).astype(np.float32)
    ang = (np.arange(S, dtype=np.float32)[:, None] * freqs[None, :]).astype(np.float32)
    cos = np.cos(ang.astype(np.float64)).astype(np.float32).T
    sin = np.sin(ang.astype(np.float64)).astype(np.float32).T
    ck = np.concatenate([cos, cos], 0)
    sk = np.concatenate([-sin, sin], 0)
    ident = np.eye(128, dtype=np.float32)
    s_i = np.arange(128)
    maskT = (s_i[:, None] // 64 <= s_i[None, :] // 64).astype(np.float32)
    pwv = np.array([2.0 ** -k for k in range(NB + 1)] + [2.0 ** (1 - k) for k in range(NB + 1)], np.float32)
    pw = np.ascontiguousarray(np.broadcast_to(pwv[None, :], (128, 2 * (NB + 1))))
    rb15 = np.ascontiguousarray(np.broadcast_to(rel_bias[15][None, :], (128, 8)))
    fg_b = np.ascontiguousarray(np.broadcast_to(final_g[None, :], (128, D)))
    NTq = S // 256
    in_maps = []
    toks = []
    for c in range(2 * B):
        b, r = c // 2, c % 2
        tok = np.concatenate([np.arange((2 * m + r) * 128, (2 * m + r + 1) * 128) for m in range(NTq)])
        toks.append(tok)
        cq = np.concatenate([np.ones((64, len(tok)), np.float32), ck[:, tok]], 0)
        sq = np.concatenate([np.zeros((64, len(tok)), np.float32), sk[:, tok]], 0)
        idx = np.stack([_t5_bucket(s_i[:, None] - s_i[None, :] + (k - 1 - r) * 128) for k in range(3)], 0)
        biasT = np.ascontiguousarray(np.transpose(rel_bias[idx], (1, 3, 0, 2))).reshape(128, 8 * 3 * 128)
        adm_st = np.stack([((2 * a + s_i[:, None] // 64) <= (2 * r + s_i[None, :] // 64)) for a in range(2)], 0)
        maskA = np.ascontiguousarray(np.transpose(adm_st, (1, 0, 2)).reshape(128, 256).astype(np.float32))
        adm = np.ascontiguousarray(np.where(np.transpose(adm_st, (2, 0, 1)).reshape(128, 256), 0.0, -1e30).astype(np.float32))
        in_maps.append({
            "xT": np.ascontiguousarray(x[b].T), "xTo": np.ascontiguousarray(x[b][tok].T), "xo": np.ascontiguousarray(x[b][tok]),
            "w_in": w_in_ext, "w_uqa": w_uq, "w_uqb": w_uqb, "w_uk": w_uk, "w_uv": w_uv,
            "w_oa": w_o_a, "w_ob": w_o_b, "w_out": w_out,
            "g_in": lay(norm_g, 8), "g_q": lay(g_q_lat, 3), "g_kv": lay(g_kv_lat, 2), "fg_b": fg_b,
            "ident": ident, "maskA": maskA, "adm": adm, "pw": pw,
            "cq": np.ascontiguousarray(cq), "sq": np.ascontiguousarray(sq),
            "ck": ck, "sk": sk, "biasT": biasT, "rb15": rb15,
        })
    key = (S, B)
    if key not in _NC_CACHE:
        _NC_CACHE[key] = build_nc(S)
    nc = _NC_CACHE[key]
    res = run_bass_kernel_spmd(nc, in_maps, core_ids=list(range(2 * B)))
    out = np.empty((B, S, D), np.float32)
    for c in range(2 * B):
        out[c // 2, toks[c]] = res.results[c]["out"]
    return out
```

```python
import numpy as np
from contextlib import ExitStack
import concourse.bass as bass
import concourse.mybir as mybir
from concourse.bass_utils import run_bass_kernel_spmd

F32 = mybir.dt.float32
BF16 = mybir.dt.bfloat16
AF = mybir.ActivationFunctionType
ALU = mybir.AluOpType
EPOCH = 20000
D = 1024
EPS = 1e-6
NB = 14
TOPK = 256.0
NEG = -30000.0
C_QLAT, C_CKV, C_KROPE, C_ZA, C_QB, C_KB, C_VB, C_ZB, C_QI, C_KI, C_WI, C_GA, C_GB = (
    0, 384, 640, 672, 1184, 1696, 2208, 2720, 3232, 3488, 3520, 3528, 4552)
C_KROT = 5576
NCOL = 5608


class Res:
    __slots__ = ("name", "lw", "rd")

    def __init__(self, name=""):
        self.name = name
        self.lw = None
        self.rd = []


class Op:
    __slots__ = ("eng", "fn", "deps", "inc", "tok", "dma", "tag")


class Ctx:
    ENG = ("pe", "act", "dve", "pool", "sp")
    NEP = {"pe": 6, "act": 3, "dve": 4, "pool": 2, "sp": 1}

    def __init__(self, nc, st):
        self.nc = nc
        self.esem = {e: [st.enter_context(nc.semaphore("s_%s%d" % (e, i))) for i in range(self.NEP[e])] for e in self.ENG}
        self.dsem_pool = [st.enter_context(nc.semaphore("s_d%d" % i)) for i in range(84)]
        self.dsem = {}
        self.dcnt = {}
        self.seq = {e: 0 for e in self.ENG}
        self.dma_hist = []

    def dma_sem(self, tag):
        if tag not in self.dsem:
            self.dsem[tag] = self.dsem_pool[len(self.dsem)]
            self.dcnt[tag] = 0
        return self.dsem[tag]


class Sched:
    def __init__(self, ctx):
        self.ctx = ctx
        self.ops = []

    def op(self, eng, fn, reads=(), writes=(), dma_tag=None):
        o = Op()
        o.eng, o.fn, o.inc, o.tok = eng, fn, False, None
        o.dma = dma_tag is not None
        o.tag = dma_tag
        idx = len(self.ops)
        deps = set()
        for r in reads:
            if r.lw is not None:
                deps.add(r.lw)
        for w in writes:
            if w.lw is not None:
                deps.add(w.lw)
            deps.update(w.rd)
        for r in reads:
            r.rd.append(idx)
        for w in writes:
            w.lw = idx
            w.rd = []
        if o.dma:
            hist = self.ctx.dma_hist
            if len(hist) >= 4 and hist[-4][0] is self:
                deps.add(hist[-4][1])
            hist.append((self, idx))
        deps.discard(idx)
        o.deps = deps
        self.ops.append(o)
        return o

    def emit(self, resources):
        import os
        ctx = self.ctx
        ctx.phase_no = getattr(ctx, "phase_no", 0) + 1
        if str(ctx.phase_no) not in os.environ.get("KPHASES", "1234"):
            for r in resources:
                r.lw = None
                r.rd = []
            return
        nc = ctx.nc
        ops = self.ops
        for o in ops:
            best = {}
            keep = set()
            for d in o.deps:
                p = ops[d]
                if p.dma:
                    keep.add(d)
                    continue
                if p.eng == "pe" and o.eng == "pe" and not o.dma:
                    continue
                if best.get(p.eng, -1) < d:
                    best[p.eng] = d
            keep.update(best.values())
            o.deps = keep
            for d in keep:
                ops[d].inc = True
            if o.dma:
                o.inc = True
        for o in ops:
            if not o.inc:
                continue
            if o.dma:
                sem = ctx.dma_sem(o.tag)
                ctx.dcnt[o.tag] += 1
                o.tok = (sem, 16 * ctx.dcnt[o.tag])
            else:
                s = ctx.seq[o.eng]
                ctx.seq[o.eng] += 1
                o.tok = (ctx.esem[o.eng][s // EPOCH], (s % EPOCH) + 1)
        per_eng = {e: [o for o in ops if o.eng == e] for e in Ctx.ENG}
        import os
        if os.environ.get("KDEBUG"):
            print("phase ops", {e: (len(per_eng[e]), sum(1 for o in per_eng[e] if o.inc)) for e in Ctx.ENG}, "dma tags", len(ctx.dsem), flush=True)
        with nc.Block() as block:
            handles = {"pe": block.tensor, "act": block.scalar, "dve": block.vector, "pool": block.gpsimd, "sp": block.sync}

            def make(e):
                def body(eng):
                    waited = {}
                    for o in per_eng[e]:
                        need = {}
                        for d in o.deps:
                            p = ops[d]
                            if p.tok is None:
                                continue
                            if p.eng == "pe" and e == "pe" and not p.dma and not o.dma:
                                continue
                            k, v = p.tok
                            if need.get(id(k), (None, 0))[1] < v:
                                need[id(k)] = (k, v)
                        for kid, (k, v) in need.items():
                            if waited.get(kid, 0) >= v:
                                continue
                            eng.wait_ge(k, v)
                            waited[kid] = v
                        ins = o.fn(eng)
                        if o.inc:
                            ins.then_inc(o.tok[0], 16 if o.dma else 1)
                    last = {}
                    for o in per_eng[e]:
                        if o.dma:
                            last[id(o.tok[0])] = o.tok
                    for kid, (k, v) in last.items():
                        if waited.get(kid, 0) < v:
                            eng.wait_ge(k, v)

                return body

            for e in Ctx.ENG:
                if per_eng[e]:
                    handles[e](make(e))
        for r in resources:
            r.lw = None
            r.rd = []


def build_nc(S):
    NG = S // 512
    NT = S // 128
    NOG = NG // 2
    NO = NOG * 512
    NOT_ = NOG * 4
    HS = min(2048, S)

    nc = bass.Bass("TRN2", target_bir_lowering=False)
    din = lambda n, s, d=F32: nc.dram_tensor(n, s, d, kind="ExternalInput").ap()
    dscr = lambda n, s, d=BF16: nc.dram_tensor(n, s, d, kind="Internal").ap()
    xT = din("xT", [D, S])
    xTo = din("xTo", [D, NO])
    maskA_d = din("maskA", [128, 256])
    adm_d = din("adm", [128, 256])
    xo = din("xo", [NO, D])
    w_in = din("w_in", [D, NCOL])
    w_uqa = din("w_uqa", [384, 768])
    w_uqb = din("w_uqb", [384, 768])
    w_uk = din("w_uk", [256, 512])
    w_uv = din("w_uv", [256, 512])
    w_oa = din("w_oa", [512, D])
    w_ob = din("w_ob", [512, D])
    w_out = din("w_out", [D, D])
    g_in = din("g_in", [128, 8])
    g_q = din("g_q", [128, 3])
    g_kv = din("g_kv", [128, 2])
    fg_b = din("fg_b", [128, D])
    ident_d = din("ident", [128, 128])
    pw_d = din("pw", [128, 2 * (NB + 1)])
    cq_d = din("cq", [96, NO])
    sq_d = din("sq", [96, NO])
    ck_d = din("ck", [32, S])
    sk_d = din("sk", [32, S])
    biasT_d = din("biasT", [128, 8 * 3 * 128])
    rb15_d = din("rb15", [128, 8])
    out_d = nc.dram_tensor("out", [NO, D], F32, kind="ExternalOutput").ap()
    knT = dscr("knT", [8, 64, S])
    kpeT = dscr("kpeT", [32, S])
    vA = dscr("vA", [S, 520])
    vB = dscr("vB", [S, 520])
    kbT = dscr("kbT", [4, 128, S])
    kiT = dscr("kiT", [32, S])
    qaT = dscr("qaT", [8, 96, NO])
    qbT = dscr("qbT", [4, 128, NO])
    qiT = dscr("qiT", [8, 32, NO])
    wiD = dscr("wiD", [NO, 8], F32)
    zg = dscr("zg", [NO, 3072])
    yaD = dscr("yaD", [NO, 512])
    ybD = dscr("ybD", [NO, 512])

    with ExitStack() as top:
        ctx = Ctx(nc, top)
        allres = []

        def R(name=""):
            x = Res(name)
            allres.append(x)
            return x

        dram_res = {k: R(k) for k in ["knT", "kpeT", "vA", "vB", "kbT", "kiT", "qaT", "qbT", "qiT", "wiD", "zg", "yaD", "ybD", "out"]}

        with ExitStack() as st:
            sb = lambda n, s, d: st.enter_context(nc.sbuf_tensor("sb_" + n, s, d))
            Sx = Sched(ctx)
            Wb = sb("Wb", [128, 8, NCOL], BF16)
            wst = [sb("wst%d" % i, [128, 768], F32) for i in range(1)] * 2
            wst_r = [R()] * 2
            Wb_r = R()
            WqA = sb("WqA", [128, 3, 768], BF16)
            WqB = sb("WqB", [128, 3, 768], BF16)
            Wk = sb("Wk", [128, 2, 512], BF16)
            Wv = sb("Wv", [128, 2, 512], BF16)
            W2_r = R()
            gin = sb("gin", [128, 8], F32)
            gq = sb("gq", [128, 3], F32)
            gkv = sb("gkv", [128, 2], F32)
            g_r = R()
            ones = sb("ones", [128, 128], BF16)
            ones_r = R()
            xs = [sb("xs0", [128, 8, 512], F32)] * 2
            xs_r = [R()] * 2
            sq = sb("sqx", [128, 8, 512], BF16)
            sq_r = R()
            rt = sb("rt", [128, 512], F32)
            rt_r = R()
            R1 = sb("R1", [128, 512], F32)
            R1_r = R()
            hT = [sb("hT0", [128, 8, 512], BF16)] * 2
            hT_r = [R()] * 2
            raw = sb("raw", [128, 3, 512], F32)
            raw_r = R()
            sq2 = sb("sqx2", [128, 3, 512], BF16)
            sq2_r = R()
            rt2 = sb("rt2", [128, 512], F32)
            rt2_r = R()
            R2 = sb("R2", [128, 512], F32)
            R2_r = R()
            ckvn = sb("ckvn", [128, 2, 512], BF16)
            ckvn_r = R()
            qln = sb("qln", [128, 3, 512], BF16)
            qln_r = R()
            knst = sb("knst", [64, 4, 512], BF16)
            knst_r = R()
            vst = [sb("vst%d" % i, [128, 4, 8, 65], BF16) for i in range(2)]
            vst_r = [R() for _ in range(2)]
            kbst = sb("kbst", [128, 4, 512], BF16)
            kbst_r = R()
            kist = sb("kist", [32, 512], BF16)
            kist_r = R()
            kpst = sb("kpst", [32, 512], BF16)
            kpst_r = R()
            tb = [sb("tb%d" % i, [96, 512], F32) for i in range(4)]
            tb_r = [R() for _ in range(4)]
            t1 = sb("t1", [96, 512], F32)
            t1_r = R()
            t2 = sb("t2", [96, 512], F32)
            t2_r = R()
            qast = sb("qast", [96, 4, 512], BF16)
            qast_r = R()
            qbst = sb("qbst", [128, 4, 512], BF16)
            qbst_r = R()
            qist = sb("qist", [128, 2, 512], BF16)
            qist_r = R()
            zgst = sb("zgst", [128, 3072], BF16)
            zgst_r = R()
            wist = sb("wist", [128, 4, 8], F32)
            wist_r = R()
            ps = st.enter_context(nc.psum_tensor("ps1", [128, 8, 512], F32))
            ps_r = [R() for _ in range(8)]
            B_SS, B_SS2 = 0, 1
            fm_banks = [2, 3, 4]
            tm_banks = [5, 6, 7]
            cnt = {"fm": 0, "tm": 0, "ev": 0}

            def nfm():
                b = fm_banks[cnt["fm"] % 3]
                cnt["fm"] += 1
                return b

            def ntm():
                b = tm_banks[cnt["tm"] % 3]
                cnt["tm"] += 1
                return b

            Sx.op("pool", lambda e: e.memset(ones[:], 1.0), writes=[ones_r])
            Sx.op("sp", lambda e: e.dma_start(out=gin[:], in_=g_in), writes=[g_r], dma_tag="g0")
            Sx.op("sp", lambda e: e.dma_start(out=gq[:], in_=g_q), writes=[g_r], dma_tag="g1")
            Sx.op("sp", lambda e: e.dma_start(out=gkv[:], in_=g_kv), writes=[g_r], dma_tag="g2")
            for i in range(2):
                for k in range(4):
                    Sx.op("pool", lambda e, i=i, k=k: e.memset(vst[i][:, k, :, :], 1.0), writes=[vst_r[i]])
            w_in_v = w_in.rearrange("(c p) n -> p c n", p=128)
            k = 0
            for c in range(8):
                for n0 in range(0, NCOL, 768):
                    n1 = min(NCOL, n0 + 768)
                    sl = k % 2
                    Sx.op("sp", lambda e, c=c, n0=n0, n1=n1, sl=sl: e.dma_start(out=wst[sl][:, 0:n1 - n0], in_=w_in_v[:, c, n0:n1]),
                          writes=[wst_r[sl]], dma_tag="wst0")
                    eng = "dve" if k % 2 == 0 else "pool"
                    Sx.op(eng, lambda e, c=c, n0=n0, n1=n1, sl=sl: e.tensor_scalar(out=Wb[:, c, n0:n1], in0=wst[sl][:, 0:n1 - n0], scalar1=gin[:, c:c + 1], scalar2=None, op0=ALU.mult),
                          reads=[wst_r[sl], g_r], writes=[Wb_r])
                    k += 1

            def small_w(dst, src, nch, ncols, gain):
                nonlocal k
                v = src.rearrange("(c p) n -> p c n", p=128)
                for c in range(nch):
                    sl = k % 2
                    Sx.op("sp", lambda e, c=c, sl=sl: e.dma_start(out=wst[sl][:, 0:ncols], in_=v[:, c, :]), writes=[wst_r[sl]], dma_tag="wst0")
                    Sx.op("dve", lambda e, c=c, sl=sl: e.tensor_scalar(out=dst[:, c, :], in0=wst[sl][:, 0:ncols], scalar1=gain[:, c:c + 1], scalar2=None, op0=ALU.mult),
                          reads=[wst_r[sl], g_r], writes=[W2_r])
                    k += 1

            small_w(WqA, w_uqa, 3, 768, gq)
            small_w(WqB, w_uqb, 3, 768, gq)
            small_w(Wk, w_uk, 2, 512, gkv)
            small_w(Wv, w_uv, 2, 512, gkv)

            xT_v = xT.rearrange("(c p) t -> p c t", p=128)

            def norm_stats(src_raw_ap_list, nchunk, sq_t, sq_res, src_res, bank, dim, rt_t, rt_res, Rt, Rt_res):
                for c in range(nchunk):
                    Sx.op("act", lambda e, c=c: e.activation(out=sq_t[:, c, :], in_=src_raw_ap_list[c], func=AF.Square), reads=[src_res], writes=[sq_res])
                for c in range(nchunk):
                    Sx.op("pe", lambda e, c=c: e.matmul(ps[:, bank, :], lhsT=ones[:], rhs=sq_t[:, c, :], start=(c == 0), stop=(c == nchunk - 1)),
                          reads=[sq_res, ones_r], writes=[ps_r[bank]])
                Sx.op("act", lambda e: e.activation(out=rt_t[:], in_=ps[:, bank, :], func=AF.Sqrt, bias=EPS, scale=1.0 / dim), reads=[ps_r[bank]], writes=[rt_res])
                Sx.op("dve", lambda e: e.reciprocal(out=Rt[:], in_=rt_t[:]), reads=[rt_res], writes=[Rt_res])

            def fm_proj(h_t, h_res, col, M, nch, Wt, Wres):
                b = nfm()
                for c in range(nch):
                    Sx.op("pe", lambda e, c=c, b=b: e.matmul(ps[0:M, b, :], lhsT=Wt[:, c, col:col + M], rhs=h_t[:, c, :], start=(c == 0), stop=(c == nch - 1)),
                          reads=[h_res, Wres], writes=[ps_r[b]])
                return b

            def evac(b, M, out_ap, out_res, extra_reads=(), scale=None):
                eng = "act" if cnt["ev"] % 2 == 0 else "dve"
                cnt["ev"] += 1
                if scale is not None:
                    Sx.op("dve", lambda e: e.tensor_scalar(out=out_ap, in0=ps[0:M, b, :], scalar1=scale, scalar2=None, op0=ALU.mult), reads=[ps_r[b]] + list(extra_reads), writes=[out_res])
                elif eng == "act":
                    Sx.op("act", lambda e: e.activation(out=out_ap, in_=ps[0:M, b, :], func=AF.Copy), reads=[ps_r[b]] + list(extra_reads), writes=[out_res])
                else:
                    Sx.op("dve", lambda e: e.tensor_copy(out=out_ap, in_=ps[0:M, b, :]), reads=[ps_r[b]] + list(extra_reads), writes=[out_res])

            def rope_combine(bA, bB, M, ctab, stab, ctab_r, stab_r, out_ap, out_res):
                Sx.op("dve", lambda e: e.tensor_tensor(out=t1[0:M, :], in0=ps[0:M, bA, :], in1=ctab[0:M, :], op=ALU.mult), reads=[ps_r[bA], ctab_r], writes=[t1_r])
                Sx.op("dve", lambda e: e.tensor_tensor(out=t2[0:M, :], in0=ps[0:M, bB, :], in1=stab[0:M, :], op=ALU.mult), reads=[ps_r[bB], stab_r], writes=[t2_r])
                Sx.op("pool", lambda e: e.tensor_tensor(out=out_ap, in0=t1[0:M, :], in1=t2[0:M, :], op=ALU.add), reads=[t1_r, t2_r], writes=[out_res])

            def q_side(H, Hr, go):
                o0 = go * 512
                for c3 in range(3):
                    b = fm_proj(H, Hr, C_QLAT + c3 * 128, 128, 8, Wb, Wb_r)
                    evac(b, 128, raw[:, c3, :], raw_r)
                norm_stats([raw[:, c, :] for c in range(3)], 3, sq2, sq2_r, raw_r, B_SS2, 384.0, rt2, rt2_r, R2, R2_r)
                for c3 in range(3):
                    Sx.op("pool", lambda e, c3=c3: e.tensor_tensor(out=qln[:, c3, :], in0=raw[:, c3, :], in1=R2[:], op=ALU.mult), reads=[raw_r, R2_r], writes=[qln_r])
                for h in range(8):
                    bA = fm_proj(qln, qln_r, h * 96, 96, 3, WqA, W2_r)
                    bB = fm_proj(qln, qln_r, h * 96, 96, 3, WqB, W2_r)
                    rope_combine(bA, bB, 96, tb[2], tb[3], tb_r[2], tb_r[3], qast[:, h % 4, :], qast_r)
                    if h % 4 == 3:
                        Sx.op("sp", lambda e, o0=o0, h=h: e.dma_start(out=qaT.rearrange("h p t -> p h t")[:, h - 3:h + 1, o0:o0 + 512], in_=qast[:]), reads=[qast_r], writes=[dram_res["qaT"]], dma_tag="qast")
                for c4 in range(4):
                    b = fm_proj(H, Hr, C_QB + c4 * 128, 128, 8, Wb, Wb_r)
                    evac(b, 128, qbst[:, c4, :], qbst_r, scale=0.125)
                Sx.op("sp", lambda e, o0=o0: e.dma_start(out=qbT.rearrange("a p t -> p a t")[:, :, o0:o0 + 512], in_=qbst[:]), reads=[qbst_r], writes=[dram_res["qbT"]], dma_tag="qbst")
                for c2 in range(2):
                    b = fm_proj(H, Hr, C_QI + c2 * 128, 128, 8, Wb, Wb_r)
                    evac(b, 128, qist[:, c2, :], qist_r)
                Sx.op("sp", lambda e, o0=o0: e.dma_start(out=qiT.rearrange("(a hh) d t -> (hh d) a t", a=2)[:, :, o0:o0 + 512], in_=qist[:]), reads=[qist_r], writes=[dram_res["qiT"]], dma_tag="qist")
                for u in range(4):
                    for (cs, k6, fn) in [(C_ZA, 0, AF.Silu), (C_ZB, 1, AF.Silu), (C_GA, 2, AF.Sigmoid), (C_GA + 512, 3, AF.Sigmoid), (C_GB, 4, AF.Sigmoid), (C_GB + 512, 5, AF.Sigmoid)]:
                        b = ntm()
                        for c in range(8):
                            Sx.op("pe", lambda e, c=c, b=b, u=u, cs=cs: e.matmul(ps[:, b, :], lhsT=H[:, c, u * 128:(u + 1) * 128], rhs=Wb[:, c, cs:cs + 512], start=(c == 0), stop=(c == 7)),
                                  reads=[Hr, Wb_r], writes=[ps_r[b]])
                        Sx.op("act", lambda e, b=b, u=u, k6=k6, fn=fn: e.activation(out=zgst[:, k6 * 512:(k6 + 1) * 512], in_=ps[:, b, :], func=fn), reads=[ps_r[b]], writes=[zgst_r])
                    Sx.op("sp", lambda e, go=go, u=u: e.dma_start(out=zg[go * 512 + u * 128:go * 512 + (u + 1) * 128, :], in_=zgst[:]), reads=[zgst_r], writes=[dram_res["zg"]], dma_tag="zgst")
                    b = ntm()
                    for c in range(8):
                        Sx.op("pe", lambda e, c=c, b=b, u=u: e.matmul(ps[:, b, 0:8], lhsT=H[:, c, u * 128:(u + 1) * 128], rhs=Wb[:, c, C_WI:C_WI + 8], start=(c == 0), stop=(c == 7)),
                              reads=[Hr, Wb_r], writes=[ps_r[b]])
                    Sx.op("dve", lambda e, b=b, u=u: e.tensor_copy(out=wist[:, u, :], in_=ps[:, b, 0:8]), reads=[ps_r[b]], writes=[wist_r])
                Sx.op("sp", lambda e, go=go: e.dma_start(out=wiD.rearrange("(n u p) c -> n p u c", u=4, p=128)[go], in_=wist[:]), reads=[wist_r], writes=[dram_res["wiD"]], dma_tag="wist")

            xTo_v = xTo.rearrange("(c p) t -> p c t", p=128)
            passes = [("k", g) for g in range(NG)] + [("q", g) for g in range(NOG)]
            for pi, (kindp, g) in enumerate(passes):
                sl = pi % 2
                t0 = g * 512
                is_own = kindp == "q"
                go = g
                srcv = xTo_v if is_own else xT_v
                Sx.op("sp", lambda e, sl=sl, t0=t0, srcv=srcv: e.dma_start(out=xs[sl][:], in_=srcv[:, :, t0:t0 + 512]), writes=[xs_r[sl]], dma_tag="xs0")
                if not is_own:
                    Sx.op("sp", lambda e, t0=t0: e.dma_start(out=tb[0][0:32, :], in_=ck_d[:, t0:t0 + 512]), writes=[tb_r[0]], dma_tag="tb0")
                    Sx.op("sp", lambda e, t0=t0: e.dma_start(out=tb[1][0:32, :], in_=sk_d[:, t0:t0 + 512]), writes=[tb_r[1]], dma_tag="tb1")
                else:
                    Sx.op("sp", lambda e, go=go: e.dma_start(out=tb[2][:], in_=cq_d[:, go * 512:go * 512 + 512]), writes=[tb_r[2]], dma_tag="tb2")
                    Sx.op("sp", lambda e, go=go: e.dma_start(out=tb[3][:], in_=sq_d[:, go * 512:go * 512 + 512]), writes=[tb_r[3]], dma_tag="tb3")
                norm_stats([xs[sl][:, c, :] for c in range(8)], 8, sq, sq_r, xs_r[sl], B_SS, float(D), rt, rt_r, R1, R1_r)
                for c in range(8):
                    eng = "dve" if c % 2 == 0 else "pool"
                    Sx.op(eng, lambda e, c=c, sl=sl: e.tensor_tensor(out=hT[sl][:, c, :], in0=xs[sl][:, c, :], in1=R1[:], op=ALU.mult), reads=[xs_r[sl], R1_r], writes=[hT_r[sl]])
                H, Hr = hT[sl], hT_r[sl]
                if is_own:
                    q_side(H, Hr, go)
                    continue
                for c2 in range(2):
                    b = fm_proj(H, Hr, C_CKV + c2 * 128, 128, 8, Wb, Wb_r)
                    evac(b, 128, raw[:, c2, :], raw_r)
                norm_stats([raw[:, c, :] for c in range(2)], 2, sq2, sq2_r, raw_r, B_SS2, 256.0, rt2, rt2_r, R2, R2_r)
                for c2 in range(2):
                    Sx.op("pool", lambda e, c2=c2: e.tensor_tensor(out=ckvn[:, c2, :], in0=raw[:, c2, :], in1=R2[:], op=ALU.mult), reads=[raw_r, R2_r], writes=[ckvn_r])
                for h in range(8):
                    b = fm_proj(ckvn, ckvn_r, h * 64, 64, 2, Wk, W2_r)
                    evac(b, 64, knst[:, h % 4, :], knst_r)
                    if h % 4 == 3:
                        Sx.op("sp", lambda e, t0=t0, h=h: e.dma_start(out=knT.rearrange("h p t -> p h t")[:, h - 3:h + 1, t0:t0 + 512], in_=knst[:]), reads=[knst_r], writes=[dram_res["knT"]], dma_tag="knst")
                for u in range(4):
                    b = ntm()
                    for c in range(2):
                        Sx.op("pe", lambda e, c=c, b=b, u=u: e.matmul(ps[:, b, :], lhsT=ckvn[:, c, u * 128:(u + 1) * 128], rhs=Wv[:, c, :], start=(c == 0), stop=(c == 1)),
                              reads=[ckvn_r, W2_r], writes=[ps_r[b]])
                    Sx.op("act", lambda e, b=b, u=u: e.activation(out=vst[0][:, u, :, 0:64], in_=ps[:, b, :].rearrange("p (h d) -> p h d", h=8), func=AF.Copy), reads=[ps_r[b]], writes=[vst_r[0]])
                Sx.op("sp", lambda e, g=g: e.dma_start(out=vA.rearrange("(n u p) c -> n p u c", u=4, p=128)[g], in_=vst[0][:].rearrange("p u h d -> p u (h d)")), reads=[vst_r[0]], writes=[dram_res["vA"]], dma_tag="vst0")
                bA = fm_proj(H, Hr, C_KROPE, 32, 8, Wb, Wb_r)
                bB = fm_proj(H, Hr, C_KROT, 32, 8, Wb, Wb_r)
                rope_combine(bA, bB, 32, tb[0], tb[1], tb_r[0], tb_r[1], kpst[:], kpst_r)
                Sx.op("sp", lambda e, t0=t0: e.dma_start(out=kpeT[:, t0:t0 + 512], in_=kpst[:]), reads=[kpst_r], writes=[dram_res["kpeT"]], dma_tag="kpst")
                for c4 in range(4):
                    b = fm_proj(H, Hr, C_KB + c4 * 128, 128, 8, Wb, Wb_r)
                    evac(b, 128, kbst[:, c4, :], kbst_r)
                Sx.op("sp", lambda e, t0=t0: e.dma_start(out=kbT.rearrange("a p t -> p a t")[:, :, t0:t0 + 512], in_=kbst[:]), reads=[kbst_r], writes=[dram_res["kbT"]], dma_tag="kbst")
                b = fm_proj(H, Hr, C_KI, 32, 8, Wb, Wb_r)
                evac(b, 32, kist[:], kist_r)
                Sx.op("sp", lambda e, t0=t0: e.dma_start(out=kiT[:, t0:t0 + 512], in_=kist[:]), reads=[kist_r], writes=[dram_res["kiT"]], dma_tag="kist")
                for u in range(4):
                    b = ntm()
                    for c in range(8):
                        Sx.op("pe", lambda e, c=c, b=b, u=u: e.matmul(ps[:, b, :], lhsT=H[:, c, u * 128:(u + 1) * 128], rhs=Wb[:, c, C_VB:C_VB + 512], start=(c == 0), stop=(c == 7)),
                              reads=[Hr, Wb_r], writes=[ps_r[b]])
                    Sx.op("act", lambda e, b=b, u=u: e.activation(out=vst[1][:, u, :, 0:64], in_=ps[:, b, :].rearrange("p (h d) -> p h d", h=8), func=AF.Copy), reads=[ps_r[b]], writes=[vst_r[1]])
                Sx.op("sp", lambda e, g=g: e.dma_start(out=vB.rearrange("(n u p) c -> n p u c", u=4, p=128)[g], in_=vst[1][:].rearrange("p u h d -> p u (h d)")), reads=[vst_r[1]], writes=[dram_res["vB"]], dma_tag="vst1")
            Sx.emit(allres)

        def load_vaug(Sx, Vt, V_r, src):
            sv = src.rearrange("(j p) c -> p j c", p=128)
            for j0 in range(0, NT, 8):
                Sx.op("sp", lambda e, j0=j0: e.dma_start(out=Vt[:, j0:j0 + 8, :], in_=sv[:, j0:j0 + 8, :]), writes=[V_r[j0 // 8]], dma_tag="V%d" % (j0 // 8))

        with ExitStack() as st:
            sb = lambda n, s, d: st.enter_context(nc.sbuf_tensor("sb_" + n, s, d))
            Sx = Sched(ctx)
            Kt = [sb("Kt%d" % i, [96, S], BF16) for i in range(2)]
            Kt_r = [R() for _ in range(2)]
            qT = [sb("qT%d" % i, [96, NO], BF16) for i in range(2)]
            qT_r = [R() for _ in range(2)]
            Vt = sb("Vt", [128, NT, 520], BF16)
            V_r = [R() for _ in range(NT // 8)]
            Pt = [sb("Pt%d" % i, [128, 512], BF16) for i in range(3)]
            Pt_r = [R() for _ in range(3)]
            mT_f = sb("mT_f", [128, 256], F32)
            mT = sb("mT", [128, 256], BF16)
            mT_r = R()
            ya = [sb("ya%d" % i, [128, 8, 512], BF16) for i in range(1)]
            ya_r = [R()]
            rec = sb("rec", [128, 1], F32)
            rec_r = R()
            ps = st.enter_context(nc.psum_tensor("ps2", [128, 8, 512], F32))
            ps_r = [R() for _ in range(8)]
            Sx.op("sp", lambda e: e.dma_start(out=mT_f[:], in_=maskA_d), writes=[mT_r], dma_tag="mT")
            Sx.op("dve", lambda e: e.tensor_copy(out=mT[:], in_=mT_f[:]), reads=[mT_r], writes=[mT_r])
            load_vaug(Sx, Vt, V_r, vA)
            sc = 96.0 ** -0.5
            gi = 0
            oi = 0
            for h in range(8):
                sl = h % 2
                for hs in range(0, S, HS):
                    Sx.op("sp", lambda e, sl=sl, h=h, hs=hs: e.dma_start(out=Kt[sl][0:64, hs:hs + HS], in_=knT[h, :, hs:hs + HS]), writes=[Kt_r[sl]], dma_tag="Kt%da" % sl)
                    Sx.op("sp", lambda e, sl=sl, hs=hs: e.dma_start(out=Kt[sl][64:96, hs:hs + HS], in_=kpeT[:, hs:hs + HS]), writes=[Kt_r[sl]], dma_tag="Kt%db" % sl)
                Sx.op("sp", lambda e, sl=sl, h=h: e.dma_start(out=qT[sl][:], in_=qaT[h]), writes=[qT_r[sl]], dma_tag="qT%d" % sl)
                for qi in range(NOT_):
                    i = 2 * qi + 1
                    ob = 6 + (oi % 2)
                    oi += 1
                    for j0 in range(0, i + 1, 4):
                        js = list(range(j0, min(i + 1, j0 + 4)))
                        sbk = gi % 6
                        psl = gi % 3
                        gi += 1
                        for jj, j in enumerate(js):
                            Sx.op("pe", lambda e, jj=jj, j=j, sl=sl, sbk=sbk, qi=qi: e.matmul(ps[:, sbk, jj * 128:(jj + 1) * 128], lhsT=Kt[sl][:, j * 128:(j + 1) * 128], rhs=qT[sl][:, qi * 128:(qi + 1) * 128], start=True, stop=True),
                                  reads=[Kt_r[sl], qT_r[sl]], writes=[ps_r[sbk]])
                        n = len(js) * 128
                        Sx.op("act", lambda e, sbk=sbk, psl=psl, n=n: e.activation(out=Pt[psl][:, 0:n], in_=ps[:, sbk, 0:n], func=AF.Exp, scale=sc), reads=[ps_r[sbk]], writes=[Pt_r[psl]])
                        for a in range(2):
                            if (i - 1 + a) in js:
                                jj = js.index(i - 1 + a)
                                Sx.op("dve", lambda e, psl=psl, jj=jj, a=a: e.tensor_tensor(out=Pt[psl][:, jj * 128:(jj + 1) * 128], in0=Pt[psl][:, jj * 128:(jj + 1) * 128], in1=mT[:, a * 128:(a + 1) * 128], op=ALU.mult),
                                      reads=[Pt_r[psl], mT_r], writes=[Pt_r[psl]])
                        for jj, j in enumerate(js):
                            Sx.op("pe", lambda e, jj=jj, j=j, psl=psl, ob=ob, h=h, i=i: e.matmul(ps[:, ob, 0:65], lhsT=Pt[psl][:, jj * 128:(jj + 1) * 128], rhs=Vt[:, j, h * 65:(h + 1) * 65], start=(j == 0), stop=(j == i)),
                                  reads=[Pt_r[psl], V_r[j // 8]], writes=[ps_r[ob]])
                    Sx.op("dve", lambda e, ob=ob: e.reciprocal(out=rec[:], in_=ps[:, ob, 64:65]), reads=[ps_r[ob]], writes=[rec_r])
                    Sx.op("dve", lambda e, ob=ob, qi=qi, h=h: e.tensor_scalar(out=ya[0][:, qi % 8, 0:64], in0=ps[:, ob, 0:64], scalar1=rec[:], scalar2=None, op0=ALU.mult),
                          reads=[ps_r[ob], rec_r], writes=[ya_r[0]])
                    if qi % 8 == 7 or qi == NOT_ - 1:
                        q8 = (qi // 8) * 8
                        nq = qi - q8 + 1
                        Sx.op("sp", lambda e, q8=q8, nq=nq, h=h: e.dma_start(out=yaD.rearrange("(q p) c -> p q c", p=128)[:, q8:q8 + nq, h * 64:(h + 1) * 64], in_=ya[0][:, 0:nq, 0:64]), reads=[ya_r[0]], writes=[dram_res["yaD"]], dma_tag="ya")
            Sx.emit(allres)

        with ExitStack() as st:
            sb = lambda n, s, d: st.enter_context(nc.sbuf_tensor("sb_" + n, s, d))
            Sx = Sched(ctx)
            Kb = sb("Kb", [128, 4, S], BF16)
            Kb_r = R()
            Vt = sb("Vt3", [128, NT, 520], BF16)
            V_r = [R() for _ in range(NT // 8)]
            Ki = sb("Ki", [64, S // 2], BF16)
            Ki_r = R()
            scs = sb("scs", [128, S], BF16)
            scs_r = R()
            nm = sb("nm", [128, S], BF16)
            nm_r = R()
            mkT = sb("mkT", [128, NT, 128], BF16)
            mkT_r = R()
            Pt = [sb("Pt3%d" % i, [128, 512], BF16) for i in range(3)]
            Pt_r = [R() for _ in range(3)]
            Rl = [sb("Rl%d" % i, [128, 2, 512], BF16) for i in range(2)]
            Rl_r = [R() for _ in range(2)]
            id_f = sb("id_f", [128, 128], F32)
            idb = sb("idb", [128, 128], BF16)
            id_r = R()
            bT_f = sb("bT_f", [128, 256], F32)
            bT = sb("bT", [128, 8, 3, 128], BF16)
            bT_r = R()
            adm = sb("adm3", [128, 256], BF16)
            adm_r = R()
            rb15 = sb("rb15", [128, 8], F32)
            rb_r = R()
            pw = sb("pw", [128, 2 * (NB + 1)], F32)
            pw_r = R()
            qb = [sb("qb%d" % i, [128, 4, 128], BF16) for i in range(2)]
            qb_r = [R() for _ in range(2)]
            qit = sb("qit", [64, 8, 128], BF16)
            qit_r = R()
            wit = sb("wit", [128, 8], F32)
            wit_r = R()
            Dg = sb("Dg", [128, 8, 128], BF16)
            Dg_r = R()
            yb = sb("yb", [128, 512], BF16)
            yb_r = R()
            nrm = sb("nrm", [128, 4], F32)
            nrm_r = R()
            sm = sb("sm", [128, 8], F32)
            sm_r = R()
            stp = sb("stp", [128, 2 * (NB + 1)], F32)
            stp_r = R()
            ps = st.enter_context(nc.psum_tensor("ps3", [128, 7, 512], F32))
            ps_r = [R() for _ in range(7)]
            pT = st.enter_context(nc.psum_tensor("pT3", [128, 1024], BF16))
            pT_r = R()
            Sx.op("sp", lambda e: e.dma_start(out=id_f[:], in_=ident_d), writes=[id_r], dma_tag="id")
            Sx.op("dve", lambda e: e.tensor_copy(out=idb[:], in_=id_f[:]), reads=[id_r], writes=[id_r])
            for q4 in range(12):
                Sx.op("sp", lambda e, q4=q4: e.dma_start(out=bT_f[:], in_=biasT_d[:, q4 * 256:(q4 + 1) * 256]), writes=[bT_r], dma_tag="bT")
                Sx.op("dve", lambda e, q4=q4: e.tensor_copy(out=bT[:].rearrange("p h k t -> p (h k t)")[:, q4 * 256:(q4 + 1) * 256], in_=bT_f[:]), reads=[bT_r], writes=[bT_r])
            Sx.op("sp", lambda e: e.dma_start(out=bT_f[:], in_=adm_d), writes=[bT_r], dma_tag="bT")
            Sx.op("dve", lambda e: e.tensor_copy(out=adm[:], in_=bT_f[:]), reads=[bT_r], writes=[adm_r])
            Sx.op("sp", lambda e: e.dma_start(out=rb15[:], in_=rb15_d), writes=[rb_r], dma_tag="rb")
            Sx.op("sp", lambda e: e.dma_start(out=pw[:], in_=pw_d), writes=[pw_r], dma_tag="pw")
            for a4 in range(4):
                for hs in range(0, S, HS):
                    Sx.op("sp", lambda e, a4=a4, hs=hs: e.dma_start(out=Kb[:, a4, hs:hs + HS], in_=kbT[a4, :, hs:hs + HS]), writes=[Kb_r], dma_tag="Kb")
            Sx.op("sp", lambda e: e.dma_start(out=Ki[0:32, :], in_=kiT[:, 0:S // 2]), writes=[Ki_r], dma_tag="Ki")
            Sx.op("sp", lambda e: e.dma_start(out=Ki[32:64, :], in_=kiT[:, S // 2:S]), writes=[Ki_r], dma_tag="Ki2")
            load_vaug(Sx, Vt, V_r, vB)
            cn = {"gi": 0, "oi": 0, "xi": 0}

            def stage_A(qi):
                i = 2 * qi + 1
                sl = qi % 2
                q0 = qi * 128
                nk = (i + 1) * 128
                Sx.op("sp", lambda e: e.dma_start(out=qb[sl][:], in_=qbT.rearrange("a p t -> p a t")[:, :, q0:q0 + 128]), writes=[qb_r[sl]], dma_tag="qb%d" % sl)
                Sx.op("sp", lambda e: e.dma_start(out=qit[0:32], in_=qiT.rearrange("h d t -> d h t")[:, :, q0:q0 + 128]), writes=[qit_r], dma_tag="qit")
                Sx.op("sp", lambda e: e.dma_start(out=qit[32:64], in_=qiT.rearrange("h d t -> d h t")[:, :, q0:q0 + 128]), writes=[qit_r], dma_tag="qjt")
                Sx.op("sp", lambda e: e.dma_start(out=wit[:], in_=wiD[q0:q0 + 128, :]), writes=[wit_r], dma_tag="wit")
                for h in range(8):
                    Sx.op("dve", lambda e, h=h: e.tensor_scalar(out=Dg[:, h, :], in0=id_f[:], scalar1=wit[:, h:h + 1], scalar2=None, op0=ALU.mult), reads=[id_r, wit_r], writes=[Dg_r])
                for k0 in range(0, nk, 512):
                    n = min(512, nk - k0)
                    for hp in range(4):
                        xb = (cn["xi"] % 2) * 2
                        rs = cn["xi"] % 2
                        cn["xi"] += 1
                        for hh in range(2):
                            h = hp * 2 + hh
                            kp = 0 if k0 < S // 2 else 32
                            kc = k0 - (0 if k0 < S // 2 else S // 2)
                            Sx.op("pe", lambda e, xb=xb, hh=hh, h=h, kc=kc, kp=kp, n=n: e.matmul(ps[:, xb + hh, 0:n], lhsT=qit[kp:kp + 32, h, :], rhs=Ki[kp:kp + 32, kc:kc + n], start=True, stop=True),
                                  reads=[qit_r, Ki_r], writes=[ps_r[xb + hh]])
                        Sx.op("act", lambda e, xb=xb, rs=rs, n=n: e.activation(out=Rl[rs][:, :, 0:n], in_=ps[:, xb:xb + 2, 0:n], func=AF.Relu), reads=[ps_r[xb], ps_r[xb + 1]], writes=[Rl_r[rs]])
                        for hh in range(2):
                            h = hp * 2 + hh
                            Sx.op("pe", lambda e, rs=rs, hh=hh, h=h, n=n: e.matmul(ps[:, 4, 0:n], lhsT=Dg[:, h, :], rhs=Rl[rs][:, hh, 0:n], start=(h == 0), stop=(h == 7)),
                                  reads=[Dg_r, Rl_r[rs]], writes=[ps_r[4]])
                    Sx.op("dve", lambda e, k0=k0, n=n: e.tensor_copy(out=scs[:, k0:k0 + n], in_=ps[:, 4, 0:n]), reads=[ps_r[4]], writes=[scs_r])

            def stage_T(qi):
                i = 2 * qi + 1
                nk = (i + 1) * 128
                if qi >= 1:
                    Sx.op("dve", lambda e: e.tensor_scalar(out=nm[:, 0:nk], in0=scs[:, 0:nk], scalar1=1.0, scalar2=-1e30, op0=ALU.mult, op1=ALU.max, accum_out=sm[:, 0:1]), reads=[scs_r], writes=[nm_r, sm_r])
                    Sx.op("dve", lambda e: e.tensor_scalar(out=nm[:, 0:nk], in0=scs[:, 0:nk], scalar1=-1.0, scalar2=-1e30, op0=ALU.mult, op1=ALU.max, accum_out=sm[:, 1:2]), reads=[scs_r], writes=[nm_r, sm_r])
                    Sx.op("dve", lambda e: e.tensor_tensor(out=sm[:, 2:3], in0=sm[:, 0:1], in1=sm[:, 1:2], op=ALU.subtract), reads=[sm_r], writes=[sm_r])
                    Sx.op("dve", lambda e: e.tensor_tensor(out=sm[:, 3:4], in0=sm[:, 0:1], in1=sm[:, 1:2], op=ALU.add), reads=[sm_r], writes=[sm_r])
                    Sx.op("dve", lambda e: e.tensor_scalar(out=sm[:, 2:4], in0=sm[:, 2:4], scalar1=0.5, scalar2=None, op0=ALU.mult), reads=[sm_r], writes=[sm_r])
                    Sx.op("dve", lambda e: e.tensor_scalar(out=stp[:], in0=pw[:], scalar1=sm[:, 3:4], scalar2=None, op0=ALU.mult), reads=[sm_r, pw_r], writes=[stp_r])
                Sx.op("dve", lambda e: e.tensor_tensor(out=scs[:, nk - 256:nk], in0=scs[:, nk - 256:nk], in1=adm[:], op=ALU.add), reads=[sm_r, adm_r, scs_r], writes=[scs_r])
                if qi >= 1:
                    for kk in range(1, NB + 1):
                        Sx.op("dve", lambda e: e.tensor_scalar(out=nm[:, 0:nk], in0=scs[:, 0:nk], scalar1=sm[:, 2:3], scalar2=0.0, op0=ALU.is_ge, op1=ALU.add, accum_out=sm[:, 4:5]), reads=[scs_r, sm_r], writes=[nm_r, sm_r])
                        Sx.op("dve", lambda e: e.tensor_scalar(out=sm[:, 5:6], in0=sm[:, 4:5], scalar1=TOPK, scalar2=0.5, op0=ALU.is_ge, op1=ALU.subtract), reads=[sm_r], writes=[sm_r])
                        Sx.op("dve", lambda e, kk=kk: e.scalar_tensor_tensor(out=sm[:, 2:3], in0=sm[:, 5:6], scalar=stp[:, NB + 1 + kk:NB + 2 + kk], in1=sm[:, 2:3], op0=ALU.mult, op1=ALU.add), reads=[sm_r, stp_r], writes=[sm_r])
                    Sx.op("dve", lambda e: e.tensor_tensor(out=sm[:, 2:3], in0=sm[:, 2:3], in1=stp[:, NB:NB + 1], op=ALU.subtract), reads=[sm_r, stp_r], writes=[sm_r])
                else:
                    Sx.op("dve", lambda e: e.memset(sm[:, 2:3], -1e29), writes=[sm_r])
                Sx.op("dve", lambda e: e.tensor_scalar(out=nm[:, 0:nk], in0=scs[:, 0:nk], scalar1=sm[:, 2:3], scalar2=None, op0=ALU.is_ge), reads=[scs_r, sm_r], writes=[nm_r])

            def stage_M(qi):
                i = 2 * qi + 1
                for j0 in range(0, i + 1, 8):
                    nj = min(8, i + 1 - j0)
                    for jj in range(nj):
                        j = j0 + jj
                        Sx.op("pe", lambda e, jj=jj, j=j: e.transpose(out=pT[:, jj * 128:(jj + 1) * 128], in_=nm[:, j * 128:(j + 1) * 128], identity=idb[:]), reads=[nm_r, id_r], writes=[pT_r])
                    Sx.op("act", lambda e, j0=j0, nj=nj: e.activation(out=mkT[:, j0:j0 + nj, :].rearrange("p j t -> p (j t)"), in_=pT[:, 0:nj * 128], func=AF.Copy), reads=[pT_r], writes=[mkT_r])

            def stage_X(qi):
                i = 2 * qi + 1
                sl = qi % 2
                q0 = qi * 128
                near0 = max(0, i - 2)
                groups = [(list(range(j0, min(near0, j0 + 4))), False) for j0 in range(0, near0, 4)] + [(list(range(near0, i + 1)), True)]
                for h in range(8):
                    ob = 5 + (cn["oi"] % 2)
                    cn["oi"] += 1
                    hp, po = h // 2, 64 * (h % 2)
                    for (js, near) in groups:
                        sbk = cn["gi"] % 4
                        psl = cn["gi"] % 3
                        cn["gi"] += 1
                        for jj, j in enumerate(js):
                            cs = slice(jj * 128, (jj + 1) * 128)
                            Sx.op("pe", lambda e, cs=cs, j=j, sbk=sbk, near=near, po=po, hp=hp: e.matmul(ps[:, sbk, cs], lhsT=Kb[po:po + 64, hp, j * 128:(j + 1) * 128], rhs=qb[sl][po:po + 64, hp, :], start=True, stop=(not near)),
                                  reads=[Kb_r, qb_r[sl]], writes=[ps_r[sbk]])
                            if near:
                                kind = j - (i - 2)
                                Sx.op("pe", lambda e, cs=cs, sbk=sbk, kind=kind, h=h: e.matmul(ps[:, sbk, cs], lhsT=idb[:], rhs=bT[:, h, kind, :], start=False, stop=True),
                                      reads=[bT_r, id_r], writes=[ps_r[sbk]])
                        n = len(js) * 128
                        if near:
                            Sx.op("act", lambda e, sbk=sbk, psl=psl, n=n: e.activation(out=Pt[psl][:, 0:n], in_=ps[:, sbk, 0:n], func=AF.Exp), reads=[ps_r[sbk]], writes=[Pt_r[psl]])
                        else:
                            Sx.op("act", lambda e, sbk=sbk, psl=psl, n=n, h=h: e.activation(out=Pt[psl][:, 0:n], in_=ps[:, sbk, 0:n], func=AF.Exp, bias=rb15[:, h:h + 1]), reads=[ps_r[sbk], rb_r], writes=[Pt_r[psl]])
                        j0 = js[0]
                        Sx.op("pool", lambda e, psl=psl, n=n, j0=j0, nj=len(js): e.tensor_tensor(out=Pt[psl][:, 0:n], in0=Pt[psl][:, 0:n], in1=mkT[:, j0:j0 + nj, :].rearrange("p j t -> p (j t)"), op=ALU.mult),
                              reads=[Pt_r[psl], mkT_r], writes=[Pt_r[psl]])
                        for jj, j in enumerate(js):
                            Sx.op("pe", lambda e, jj=jj, j=j, psl=psl, ob=ob, h=h: e.matmul(ps[:, ob, 0:65], lhsT=Pt[psl][:, jj * 128:(jj + 1) * 128], rhs=Vt[:, j, h * 65:(h + 1) * 65], start=(j == 0), stop=(j == i)),
                                  reads=[Pt_r[psl], V_r[j // 8]], writes=[ps_r[ob]])
                    Sx.op("act", lambda e, ob=ob: e.activation(out=nrm[:, 0:1], in_=ps[:, ob, 64:65], func=AF.Ln), reads=[ps_r[ob]], writes=[nrm_r])
                    Sx.op("act", lambda e: e.activation(out=nrm[:, 1:2], in_=nrm[:, 0:1], func=AF.Exp, scale=-1.0), reads=[nrm_r], writes=[nrm_r])
                    Sx.op("act", lambda e, ob=ob, h=h: e.activation(out=yb[:, h * 64:(h + 1) * 64], in_=ps[:, ob, 0:64], func=AF.Copy, scale=nrm[:, 1:2]), reads=[ps_r[ob], nrm_r], writes=[yb_r])
                Sx.op("sp", lambda e: e.dma_start(out=ybD[q0:q0 + 128, :], in_=yb[:]), reads=[yb_r], writes=[dram_res["ybD"]], dma_tag="yb")

            stage_A(0)
            stage_T(0)
            stage_M(0)
            for qi in range(NOT_):
                if qi + 1 < NOT_:
                    stage_A(qi + 1)
                    stage_T(qi + 1)
                stage_X(qi)
                if qi + 1 < NOT_:
                    stage_M(qi + 1)
            Sx.emit(allres)

        with ExitStack() as st:
            sb = lambda n, s, d: st.enter_context(nc.sbuf_tensor("sb_" + n, s, d))
            Sx = Sched(ctx)
            wst = [sb("wst4%d" % i, [128, 1024], F32) for i in range(2)]
            wst_r = [R() for _ in range(2)]
            Woa = sb("Woa", [128, 4, D], BF16)
            Wob = sb("Wob", [128, 4, D], BF16)
            Wo = sb("Wo", [128, 8, D], BF16)
            W_r = R()
            fg = sb("fg", [128, D], F32)
            fg_r = R()
            id_f = sb("id_f4", [128, 128], F32)
            idb = sb("idb4", [128, 128], BF16)
            id_r = R()
            yat = [sb("yat%d" % i, [128, 512], BF16) for i in range(2)]
            ybt = [sb("ybt%d" % i, [128, 512], BF16) for i in range(2)]
            zgt = [sb("zgt%d" % i, [128, 3072], BF16) for i in range(2)]
            xt = [sb("xt%d" % i, [128, D], F32) for i in range(2)]
            in_r = [R() for _ in range(2)]
            gab = sb("gab", [128, 2, 512], BF16)
            gab_r = R()
            gT = sb("gT", [128, 8, 128], BF16)
            gT_r = R()
            m1 = sb("m1", [128, D], F32)
            m1_r = R()
            m2 = sb("m2", [128, D], F32)
            m2_r = R()
            mg = sb("mg", [128, D], BF16)
            mg_r = R()
            mgT = sb("mgT", [128, 8, 128], BF16)
            mgT_r = R()
            xo_t = sb("xo_t", [128, D], F32)
            xo_r = R()
            junk = sb("junk4", [128, D], F32)
            junk_r = R()
            ssq = sb("ssq", [128, 2], F32)
            ssq_r = R()
            ot = [sb("ot%d" % i, [128, D], F32) for i in range(2)]
            ot_r = [R() for _ in range(2)]
            ps = st.enter_context(nc.psum_tensor("ps4", [128, 6, 512], F32))
            ps_r = [R() for _ in range(6)]
            pT = st.enter_context(nc.psum_tensor("pT4", [128, 2, 1024], BF16))
            pT_r = [R() for _ in range(2)]
            k = 0
            for (dst, src, nch) in [(Woa, w_oa, 4), (Wob, w_ob, 4), (Wo, w_out, 8)]:
                v = src.rearrange("(c p) n -> p c n", p=128)
                for c in range(nch):
                    sl = k % 2
                    Sx.op("sp", lambda e, c=c, sl=sl, v=v: e.dma_start(out=wst[sl][:], in_=v[:, c, :]), writes=[wst_r[sl]], dma_tag="w4%d" % sl)
                    Sx.op("dve" if k % 2 == 0 else "pool", lambda e, c=c, sl=sl, dst=dst: e.tensor_copy(out=dst[:, c, :], in_=wst[sl][:]), reads=[wst_r[sl]], writes=[W_r])
                    k += 1
            Sx.op("sp", lambda e: e.dma_start(out=fg[:], in_=fg_b), writes=[fg_r], dma_tag="fg")
            Sx.op("sp", lambda e: e.dma_start(out=id_f[:], in_=ident_d), writes=[id_r], dma_tag="id4")
            Sx.op("dve", lambda e: e.tensor_copy(out=idb[:], in_=id_f[:]), reads=[id_r], writes=[id_r])
            for qi in range(NOT_):
                sl = qi % 2
                q0 = qi * 128
                Sx.op("sp", lambda e, sl=sl, q0=q0: e.dma_start(out=yat[sl][:], in_=yaD[q0:q0 + 128, :]), reads=[dram_res["yaD"]], writes=[in_r[sl]], dma_tag="i4a%d" % sl)
                Sx.op("sp", lambda e, sl=sl, q0=q0: e.dma_start(out=ybt[sl][:], in_=ybD[q0:q0 + 128, :]), reads=[dram_res["ybD"]], writes=[in_r[sl]], dma_tag="i4b%d" % sl)
                Sx.op("sp", lambda e, sl=sl, q0=q0: e.dma_start(out=zgt[sl][:], in_=zg[q0:q0 + 128, :]), reads=[dram_res["zg"]], writes=[in_r[sl]], dma_tag="i4c%d" % sl)
                Sx.op("sp", lambda e, sl=sl, q0=q0: e.dma_start(out=xt[sl][:], in_=xo[q0:q0 + 128, :]), writes=[in_r[sl]], dma_tag="i4d%d" % sl)
                Sx.op("dve", lambda e, sl=sl: e.tensor_tensor(out=gab[:, 0, :], in0=yat[sl][:], in1=zgt[sl][:, 0:512], op=ALU.mult), reads=[in_r[sl]], writes=[gab_r])
                Sx.op("pool", lambda e, sl=sl: e.tensor_tensor(out=gab[:, 1, :], in0=ybt[sl][:], in1=zgt[sl][:, 512:1024], op=ALU.mult), reads=[in_r[sl]], writes=[gab_r])
                for m in range(8):
                    Sx.op("pe", lambda e, m=m: e.transpose(out=pT[:, 0, m * 128:(m + 1) * 128], in_=gab[:, m // 4, (m % 4) * 128:(m % 4 + 1) * 128], identity=idb[:]), reads=[gab_r, id_r], writes=[pT_r[0]])
                Sx.op("act", lambda e: e.activation(out=gT[:].rearrange("p m t -> p (m t)"), in_=pT[:, 0, :], func=AF.Copy), reads=[pT_r[0]], writes=[gT_r])
                for ab in range(2):
                    Wt = Woa if ab == 0 else Wob
                    for hf in range(2):
                        b = ab * 2 + hf
                        for c in range(4):
                            Sx.op("pe", lambda e, b=b, c=c, ab=ab, hf=hf, Wt=Wt: e.matmul(ps[:, b, :], lhsT=gT[:, ab * 4 + c, :], rhs=Wt[:, c, hf * 512:(hf + 1) * 512], start=(c == 0), stop=(c == 3)),
                                  reads=[gT_r, W_r], writes=[ps_r[b]])
                for hf in range(2):
                    Sx.op("dve", lambda e, hf=hf, sl=sl: e.tensor_tensor(out=m1[:, hf * 512:(hf + 1) * 512], in0=ps[:, hf, :], in1=zgt[sl][:, 1024 + hf * 512:1536 + hf * 512], op=ALU.mult), reads=[ps_r[hf], in_r[sl]], writes=[m1_r])
                    Sx.op("dve", lambda e, hf=hf, sl=sl: e.tensor_tensor(out=m2[:, hf * 512:(hf + 1) * 512], in0=ps[:, 2 + hf, :], in1=zgt[sl][:, 2048 + hf * 512:2560 + hf * 512], op=ALU.mult), reads=[ps_r[2 + hf], in_r[sl]], writes=[m2_r])
                Sx.op("pool", lambda e: e.tensor_tensor(out=mg[:], in0=m1[:], in1=m2[:], op=ALU.add), reads=[m1_r, m2_r], writes=[mg_r])
                for m in range(8):
                    Sx.op("pe", lambda e, m=m: e.transpose(out=pT[:, 1, m * 128:(m + 1) * 128], in_=mg[:, m * 128:(m + 1) * 128], identity=idb[:]), reads=[mg_r, id_r], writes=[pT_r[1]])
                Sx.op("act", lambda e: e.activation(out=mgT[:].rearrange("p m t -> p (m t)"), in_=pT[:, 1, :], func=AF.Copy), reads=[pT_r[1]], writes=[mgT_r])
                for hf in range(2):
                    b = 4 + hf
                    for c in range(8):
                        Sx.op("pe", lambda e, b=b, c=c, hf=hf: e.matmul(ps[:, b, :], lhsT=mgT[:, c, :], rhs=Wo[:, c, hf * 512:(hf + 1) * 512], start=(c == 0), stop=(c == 7)),
                              reads=[mgT_r, W_r], writes=[ps_r[b]])
                    Sx.op("dve", lambda e, b=b, hf=hf, sl=sl: e.tensor_tensor(out=xo_t[:, hf * 512:(hf + 1) * 512], in0=ps[:, b, :], in1=xt[sl][:, hf * 512:(hf + 1) * 512], op=ALU.add), reads=[ps_r[b], in_r[sl]], writes=[xo_r])
                Sx.op("act", lambda e: e.activation(out=junk[:], in_=xo_t[:], func=AF.Square, accum_out=ssq[:, 0:1]), reads=[xo_r], writes=[junk_r, ssq_r])
                Sx.op("act", lambda e: e.activation(out=ssq[:, 1:2], in_=ssq[:, 0:1], func=AF.Sqrt, bias=EPS, scale=1.0 / D), reads=[ssq_r], writes=[ssq_r])
                Sx.op("dve", lambda e: e.reciprocal(out=ssq[:, 1:2], in_=ssq[:, 1:2]), reads=[ssq_r], writes=[ssq_r])
                Sx.op("dve", lambda e, sl=sl: e.scalar_tensor_tensor(out=ot[sl][:], in0=xo_t[:], scalar=ssq[:, 1:2], in1=fg[:], op0=ALU.mult, op1=ALU.mult), reads=[xo_r, ssq_r, fg_r], writes=[ot_r[sl]])
                Sx.op("sp", lambda e, sl=sl, q0=q0: e.dma_start(out=out_d[q0:q0 + 128, :], in_=ot[sl][:]), reads=[ot_r[sl]], writes=[dram_res["out"]], dma_tag="o4%d" % sl)
            Sx.emit(allres)
    return nc


def _t5_bucket(rel):
    nb, max_exact = 16, 8
    ret = (rel > 0).astype(np.int32) * nb
    n = np.abs(rel)
    nf = np.maximum(n, 1).astype(np.float32)
    large = max_exact + (np.log(nf / np.float32(max_exact)) / np.float32(np.log(128 / max_exact)) * np.float32(nb - max_exact)).astype(np.int32)
    large = np.minimum(large, nb - 1)
    return ret + np.where(n < max_exact, n, large)


_NC_CACHE = {}


def kernel(x, norm_g, w_in, g_q_lat, w_uq, g_kv_lat, w_ukv, w_o_a, w_o_b, w_out, rel_bias, final_g):
    x = np.asarray(x, np.float32)
    B, S, _ = x.shape
    f = lambda a: np.ascontiguousarray(np.asarray(a, np.float32))
    norm_g, w_in, g_q_lat, w_uq, g_kv_lat, w_ukv = f(norm_g)[0], f(w_in)[0], f(g_q_lat)[0], f(w_uq)[0], f(g_kv_lat)[0], f(w_ukv)[0]
    w_o_a, w_o_b, w_out, rel_bias, final_g = f(w_o_a)[0], f(w_o_b)[0], f(w_out)[0], f(rel_bias), f(final_g)
    NG = S // 512
    perm = np.concatenate([np.arange(16, 32), np.arange(0, 16)])
    w_in_ext = np.concatenate([w_in, w_in[:, C_KROPE:C_KROPE + 32][:, perm]], axis=1)
    w_uqb = np.zeros_like(w_uq)
    for h in range(8):
        w_uqb[:, h * 96 + 64:h * 96 + 96] = w_uq[:, h * 96 + 64:h * 96 + 96][:, perm]
    w3 = w_ukv.reshape(256, 8, 128)
    w_uk = np.ascontiguousarray(w3[:, :, :64].reshape(256, 512))
    w_uv = np.ascontiguousarray(w3[:, :, 64:].reshape(256, 512))
    lay = lambda g, c: np.ascontiguousarray(g.reshape(c, 128).T)
    half = 16
    freqs = (np.float32(10000.0) ** (-np.arange(half, dtype=np.float32) / np.float32(half))).astype(np.float32)
    ang = (np.arange(S, dtype=np.float32)[:, None] * freqs[None, :]).astype(np.float32)
    cos = np.cos(ang.astype(np.float64)).astype(np.float32).T
    sin = np.sin(ang.astype(np.float64)).astype(np.float32).T
    ck = np.concatenate([cos, cos], 0)
    sk = np.concatenate([-sin, sin], 0)
    ident = np.eye(128, dtype=np.float32)
    s_i = np.arange(128)
    maskT = (s_i[:, None] // 64 <= s_i[None, :] // 64).astype(np.float32)
    pwv = np.array([2.0 ** -k for k in range(NB + 1)] + [2.0 ** (1 - k) for k in range(NB + 1)], np.float32)
    pw = np.ascontiguousarray(np.broadcast_to(pwv[None, :], (128, 2 * (NB + 1))))
    rb15 = np.ascontiguousarray(np.broadcast_to(rel_bias[15][None, :], (128, 8)))
    fg_b = np.ascontiguousarray(np.broadcast_to(final_g[None, :], (128, D)))
    NTq = S // 256
    in_maps = []
    toks = []
    for c in range(2 * B):
        b, r = c // 2, c % 2
        tok = np.concatenate([np.arange((2 * m + r) * 128, (2 * m + r + 1) * 128) for m in range(NTq)])
        toks.append(tok)
        cq = np.concatenate([np.ones((64, len(tok)), np.float32), ck[:, tok]], 0)
        sq = np.concatenate([np.zeros((64, len(tok)), np.float32), sk[:, tok]], 0)
        idx = np.stack([_t5_bucket(s_i[:, None] - s_i[None, :] + (k - 1 - r) * 128) for k in range(3)], 0)
        biasT = np.ascontiguousarray(np.transpose(rel_bias[idx], (1, 3, 0, 2))).reshape(128, 8 * 3 * 128)
        adm_st = np.stack([((2 * a + s_i[:, None] // 64) <= (2 * r + s_i[None, :] // 64)) for a in range(2)], 0)
        maskA = np.ascontiguousarray(np.transpose(adm_st, (1, 0, 2)).reshape(128, 256).astype(np.float32))
        adm = np.ascontiguousarray(np.where(np.transpose(adm_st, (2, 0, 1)).reshape(128, 256), 0.0, -1e30).astype(np.float32))
        in_maps.append({
            "xT": np.ascontiguousarray(x[b].T), "xTo": np.ascontiguousarray(x[b][tok].T), "xo": np.ascontiguousarray(x[b][tok]),
            "w_in": w_in_ext, "w_uqa": w_uq, "w_uqb": w_uqb, "w_uk": w_uk, "w_uv": w_uv,
            "w_oa": w_o_a, "w_ob": w_o_b, "w_out": w_out,
            "g_in": lay(norm_g, 8), "g_q": lay(g_q_lat, 3), "g_kv": lay(g_kv_lat, 2), "fg_b": fg_b,
            "ident": ident, "maskA": maskA, "adm": adm, "pw": pw,
            "cq": np.ascontiguousarray(cq), "sq": np.ascontiguousarray(sq),
            "ck": ck, "sk": sk, "biasT": biasT, "rb15": rb15,
        })
    key = (S, B)
    if key not in _NC_CACHE:
        _NC_CACHE[key] = build_nc(S)
    nc = _NC_CACHE[key]
    res = run_bass_kernel_spmd(nc, in_maps, core_ids=list(range(2 * B)))
    out = np.empty((B, S, D), np.float32)
    for c in range(2 * B):
        out[c // 2, toks[c]] = res.results[c]["out"]
    return out
```
